# Optimizing a Trainium2 kernel written in Bass

```python
import math
import jax
import jax.numpy as jnp
from jax import lax
import numpy as np

D_MODEL = 4096
BATCH = 2
SEQ = 8192
DEPTH = 1
DEC_BATCH = 16
DEC_SEQ = 16
PAST_LEN = 1024

CHUNK = 64
MIX_WIDTH = D_MODEL
HD_DIFF = 128
H_DIFF = (MIX_WIDTH // 2) // (2 * HD_DIFF)
H_GLA = 4
GLA_DK = (MIX_WIDTH // 4) // H_GLA
GLA_DV = (MIX_WIDTH // 2) // H_GLA
GATE_RANK = 16
GLA_TAU = 16.0
ROPE_THETA = 10000.0
Q_BLOCK = 128
N_KEYS = 128
N_EXPERTS = N_KEYS * N_KEYS
PEER_HEADS = 8
PEER_DKEY = 256
PEER_HALF = PEER_DKEY // 2
PEER_TOPK = 16
PEER_BLOCK = 128
ALPHA = (2 * DEPTH) ** 0.25
BETA = (8 * DEPTH) ** -0.25
EPS = 1e-5
NEG_INF = -1e30

DIFF_QK = H_DIFF * 2 * HD_DIFF
DIFF_V = H_DIFF * 2 * HD_DIFF
GLA_QK = H_GLA * GLA_DK
GLA_V = H_GLA * GLA_DV
SPLIT_SIZES = (DIFF_QK, DIFF_QK, DIFF_V, GLA_QK, GLA_QK, GLA_V, GLA_V, GATE_RANK)
IN_COLS = sum(SPLIT_SIZES)

kernel_name = "hybrid_diffattn_gla_peer_stream_step"


def layernorm(x, g, b):
    xf = x.astype(jnp.float32)
    mu = jnp.mean(xf, axis=-1, keepdims=True)
    var = jnp.mean(jnp.square(xf - mu), axis=-1, keepdims=True)
    return ((xf - mu) * lax.rsqrt(var + EPS)).astype(x.dtype) * g + b


def rmsnorm(x, g):
    xf = x.astype(jnp.float32)
    return (xf * lax.rsqrt(jnp.mean(jnp.square(xf), axis=-1, keepdims=True) + EPS)).astype(x.dtype) * g


def rope(x, pos):
    d = x.shape[-1]
    inv = ROPE_THETA ** (-jnp.arange(0, d, 2, dtype=jnp.float32) / d)
    ang = pos.astype(jnp.float32)[:, None] * inv[None, :]
    ang = jnp.concatenate([ang, ang], axis=-1)
    cos = jnp.cos(ang)[:, None, None, :].astype(x.dtype)
    sin = jnp.sin(ang)[:, None, None, :].astype(x.dtype)
    x1, x2 = jnp.split(x, 2, axis=-1)
    return x * cos + jnp.concatenate([-x2, x1], axis=-1) * sin


def diff_attention(q, qpos, k, v, kpos, lam):
    B, T = q.shape[:2]
    qblk = Q_BLOCK if T % Q_BLOCK == 0 else T
    nblk = T // qblk
    qs = q.reshape(B, nblk, qblk, H_DIFF, 2, HD_DIFF).swapaxes(0, 1)
    qp = qpos.reshape(nblk, qblk)
    kchunk = kpos // CHUNK
    scale = HD_DIFF ** -0.5

    def block(args):
        qb, pb = args
        s = jnp.einsum('bqhid,bkhid->bihqk', qb, k).astype(jnp.float32) * scale
        mask = (pb // CHUNK)[:, None] >= kchunk[None, :]
        p = jax.nn.softmax(jnp.where(mask, s, NEG_INF), axis=-1)
        w = p[:, 0] - lam * p[:, 1]
        return jnp.einsum('bhqk,bkhe->bqhe', w.astype(v.dtype), v)

    o = lax.map(block, (qs, qp))
    return o.swapaxes(0, 1).reshape(B, T, H_DIFF, 2 * HD_DIFF)


def gla_chunked(q, k, v, logf, s0, L):
    B, T, H, dk = q.shape
    dv = v.shape[-1]
    n = T // L
    scale = dk ** -0.5
    causal = jnp.tril(jnp.ones((L, L), dtype=bool))

    def to_chunks(a):
        return a.astype(jnp.float32).reshape(B, n, L, H, a.shape[-1]).transpose(1, 0, 3, 2, 4)

    def step(S, inp):
        qc, kc, vc, gc = inp
        b = jnp.cumsum(gc, axis=-2)
        b_last = b[..., -1:, :]
        q_in = qc * jnp.exp(b) * scale
        k_in = kc * jnp.exp(-b)
        k_end = kc * jnp.exp(b_last - b)
        a = jnp.where(causal, jnp.einsum('bhtk,bhsk->bhts', q_in, k_in), 0.0)
        o = jnp.einsum('bhts,bhsv->bhtv', a, vc) + jnp.einsum('bhtk,bhkv->bhtv', q_in, S)
        S = jnp.exp(b_last).swapaxes(-1, -2) * S + jnp.einsum('bhsk,bhsv->bhkv', k_end, vc)
        return S, o

    S, o = lax.scan(step, s0.astype(jnp.float32), (to_chunks(q), to_chunks(k), to_chunks(v), to_chunks(logf)))
    o = o.transpose(1, 0, 3, 2, 4).reshape(B, T, H, dv)
    return o, S


def peer(x, w_q, keys1, keys2, u_tab, v_tab):
    B, T, D = x.shape
    n = B * T
    blk = min(PEER_BLOCK, n)
    nb = -(-n // blk)
    xt = jnp.pad(x.reshape(n, D), ((0, nb * blk - n), (0, 0))).reshape(nb, blk, D)

    def one(xb):
        q = (xb @ w_q).reshape(blk, PEER_HEADS, 2, PEER_HALF)
        s1 = jnp.einsum('nhd,hkd->nhk', q[:, :, 0], keys1).astype(jnp.float32)
        s2 = jnp.einsum('nhd,hkd->nhk', q[:, :, 1], keys2).astype(jnp.float32)
        t1, i1 = lax.top_k(s1, PEER_TOPK)
        t2, i2 = lax.top_k(s2, PEER_TOPK)
        cand = (t1[..., :, None] + t2[..., None, :]).reshape(blk, PEER_HEADS, PEER_TOPK * PEER_TOPK)
        cidx = (i1[..., :, None] * N_KEYS + i2[..., None, :]).reshape(blk, PEER_HEADS, PEER_TOPK * PEER_TOPK)
        best, sel = lax.top_k(cand, PEER_TOPK)
        idx = jnp.take_along_axis(cidx, sel, axis=-1)
        g = jax.nn.softmax(best, axis=-1)
        u = jnp.take(u_tab, idx, axis=0)
        act = jax.nn.gelu(jnp.einsum('nd,nhkd->nhk', xb, u), approximate=False)
        coeff = (g * act.astype(jnp.float32)).astype(xb.dtype)
        return jnp.einsum('nhk,nhkd->nd', coeff, jnp.take(v_tab, idx, axis=0))

    y = lax.map(one, xt).reshape(nb * blk, D)[:n]
    return y.reshape(B, T, D)


def encoder_layer(x, pos, past_k, past_v, gla_s0, gla_chunk, lam_init,
                  w_in, w_gate2, b_gate, lam_q1, lam_k1, lam_q2, lam_k2,
                  diff_norm_g, gla_norm_g, w_out, ln1_g, ln1_b, ln2_g, ln2_b,
                  peer_wq, peer_keys1, peer_keys2, peer_u, peer_v):
    B, T, _ = x.shape
    proj = x @ w_in
    points = np.cumsum(SPLIT_SIZES)[:-1].tolist()
    dq, dk, dv, gq, gk, gv, gr, gz = jnp.split(proj, points, axis=-1)

    dq = rope(dq.reshape(B, T, H_DIFF, 2, HD_DIFF), pos)
    dk = rope(dk.reshape(B, T, H_DIFF, 2, HD_DIFF), pos)
    dv = dv.reshape(B, T, H_DIFF, 2 * HD_DIFF)
    new_k = dk.reshape(B, T, H_DIFF, 2 * HD_DIFF)
    new_v = dv
    if past_k is None:
        keys, vals, kpos = dk, dv, pos
    else:
        P = past_k.shape[1]
        keys = jnp.concatenate([past_k.reshape(B, P, H_DIFF, 2, HD_DIFF).astype(dk.dtype), dk], axis=1)
        vals = jnp.concatenate([past_v.astype(dv.dtype), dv], axis=1)
        kpos = jnp.arange(P + T, dtype=jnp.int32)
    lam = (jnp.exp(jnp.sum(lam_q1.astype(jnp.float32) * lam_k1.astype(jnp.float32)))
           - jnp.exp(jnp.sum(lam_q2.astype(jnp.float32) * lam_k2.astype(jnp.float32))) + lam_init)
    o_diff = diff_attention(dq, pos, keys, vals, kpos, lam)
    o_diff = rmsnorm(o_diff, diff_norm_g) * (1.0 - lam_init)

    logf = jax.nn.log_sigmoid((gz @ w_gate2 + b_gate).astype(jnp.float32)) / GLA_TAU
    o_gla, s_new = gla_chunked(gq.reshape(B, T, H_GLA, GLA_DK), gk.reshape(B, T, H_GLA, GLA_DK),
                               gv.reshape(B, T, H_GLA, GLA_DV), logf.reshape(B, T, H_GLA, GLA_DK),
                               gla_s0, gla_chunk)
    o_gla = rmsnorm(o_gla.astype(x.dtype), gla_norm_g) * jax.nn.silu(gr.reshape(B, T, H_GLA, GLA_DV))

    mix = jnp.concatenate([o_diff.reshape(B, T, DIFF_V), o_gla.reshape(B, T, GLA_V)], axis=-1) @ w_out
    x1 = layernorm(ALPHA * x + mix, ln1_g, ln1_b)
    x2 = layernorm(ALPHA * x1 + peer(x1, peer_wq, peer_keys1, peer_keys2, peer_u, peer_v), ln2_g, ln2_b)
    return x2, new_k, new_v, s_new.astype(x.dtype)


def setup_inputs(seed: int = 0) -> dict:
    key = jax.random.key(seed)
    ks = jax.random.split(key, 26)
    nrm = jax.random.normal
    f32 = jnp.float32
    return {
        "x_prompt": nrm(ks[0], (BATCH, SEQ, D_MODEL), f32),
        "x_sample": nrm(ks[1], (DEC_BATCH, DEC_SEQ, D_MODEL), f32),
        "cache_diff_k": nrm(ks[2], (DEPTH, DEC_BATCH, PAST_LEN, H_DIFF, 2 * HD_DIFF), f32),
        "cache_diff_v": nrm(ks[3], (DEPTH, DEC_BATCH, PAST_LEN, H_DIFF, 2 * HD_DIFF), f32),
        "state_gla": nrm(ks[4], (DEPTH, DEC_BATCH, H_GLA, GLA_DK, GLA_DV), f32),
        "w_in": nrm(ks[5], (DEPTH, D_MODEL, IN_COLS), f32) * D_MODEL ** -0.5,
        "w_gate2": nrm(ks[6], (DEPTH, GATE_RANK, GLA_QK), f32) * GATE_RANK ** -0.5,
        "b_gate": nrm(ks[7], (DEPTH, GLA_QK), f32) * 0.1,
        "lam_q1": nrm(ks[8], (DEPTH, HD_DIFF), f32) * 0.1,
        "lam_k1": nrm(ks[9], (DEPTH, HD_DIFF), f32) * 0.1,
        "lam_q2": nrm(ks[10], (DEPTH, HD_DIFF), f32) * 0.1,
        "lam_k2": nrm(ks[11], (DEPTH, HD_DIFF), f32) * 0.1,
        "diff_norm_g": 1.0 + 0.01 * nrm(ks[12], (DEPTH, 2 * HD_DIFF), f32),
        "gla_norm_g": 1.0 + 0.01 * nrm(ks[13], (DEPTH, GLA_DV), f32),
        "w_out": nrm(ks[14], (DEPTH, MIX_WIDTH, D_MODEL), f32) * (MIX_WIDTH ** -0.5 * BETA),
        "ln1_g": 1.0 + 0.01 * nrm(ks[15], (DEPTH, D_MODEL), f32),
        "ln1_b": 0.01 * nrm(ks[16], (DEPTH, D_MODEL), f32),
        "ln2_g": 1.0 + 0.01 * nrm(ks[17], (DEPTH, D_MODEL), f32),
        "ln2_b": 0.01 * nrm(ks[18], (DEPTH, D_MODEL), f32),
        "peer_wq": nrm(ks[19], (DEPTH, D_MODEL, PEER_HEADS * PEER_DKEY), f32) * D_MODEL ** -0.5,
        "peer_keys1": nrm(ks[20], (DEPTH, PEER_HEADS, N_KEYS, PEER_HALF), f32) * PEER_HALF ** -0.5,
        "peer_keys2": nrm(ks[21], (DEPTH, PEER_HEADS, N_KEYS, PEER_HALF), f32) * PEER_HALF ** -0.5,
        "peer_u": nrm(ks[22], (DEPTH, N_EXPERTS, D_MODEL), f32) * D_MODEL ** -0.5,
        "peer_v": nrm(ks[23], (DEPTH, N_EXPERTS, D_MODEL), f32) * (BETA * PEER_HEADS ** -0.5),
    }


def reference(x_prompt, x_sample, cache_diff_k, cache_diff_v, state_gla,
              w_in, w_gate2, b_gate, lam_q1, lam_k1, lam_q2, lam_k2,
              diff_norm_g, gla_norm_g, w_out, ln1_g, ln1_b, ln2_g, ln2_b,
              peer_wq, peer_keys1, peer_keys2, peer_u, peer_v):
    Bp, Tp = x_prompt.shape[:2]
    Ts = x_sample.shape[1]
    P = cache_diff_k.shape[2]
    pos_p = jnp.arange(Tp, dtype=jnp.int32)
    pos_s = P + jnp.arange(Ts, dtype=jnp.int32)
    y_p, y_s = x_prompt, x_sample
    kp_list, vp_list, sp_list, ks_list, vs_list, ss_list = [], [], [], [], [], []
    for l in range(DEPTH):
        lam_init = 0.8 - 0.6 * math.exp(-0.3 * l)
        params = (w_in[l], w_gate2[l], b_gate[l], lam_q1[l], lam_k1[l], lam_q2[l], lam_k2[l],
                  diff_norm_g[l], gla_norm_g[l], w_out[l], ln1_g[l], ln1_b[l], ln2_g[l], ln2_b[l],
                  peer_wq[l], peer_keys1[l], peer_keys2[l], peer_u[l], peer_v[l])
        s0 = jnp.zeros((Bp, H_GLA, GLA_DK, GLA_DV), jnp.float32)
        y_p, kp, vp, sp = encoder_layer(y_p, pos_p, None, None, s0, CHUNK, lam_init, *params)
        y_s, k_s, v_s, s_s = encoder_layer(y_s, pos_s, cache_diff_k[l], cache_diff_v[l], state_gla[l],
                                           Ts, lam_init, *params)
        kp_list.append(kp)
        vp_list.append(vp)
        sp_list.append(sp)
        ks_list.append(k_s)
        vs_list.append(v_s)
        ss_list.append(s_s)
    return (y_p, y_s, jnp.stack(kp_list), jnp.stack(vp_list), jnp.stack(sp_list),
            jnp.stack(ks_list), jnp.stack(vs_list), jnp.stack(ss_list))
```

```python
import math
from contextlib import ExitStack

import numpy as np
import concourse.bass as bass
import concourse.mybir as mybir
from concourse.bass_utils import run_bass_kernel_spmd

F32 = mybir.dt.float32
BF16 = mybir.dt.bfloat16
ALU = mybir.AluOpType
AF = mybir.ActivationFunctionType
AX = mybir.AxisListType

D = 4096
NCORE = 8
NSB = 16
TS = 16
PAST = 1024
NS = NSB * TS
EPS = 1e-5
ALPHA = 2.0 ** 0.25
LAM_INIT = 0.8 - 0.6 * math.exp(0.0)
NEXP = 16384
SCALE_D = 128 ** -0.5
SCALE_G = 256 ** -0.5


class Buf:
    __slots__ = ("name", "lw", "rd")

    def __init__(self, name=""):
        self.name = name
        self.lw = None
        self.rd = []


class Sched:
    def __init__(self, nc, es, ndma=8):
        self.nc = nc
        self.eng = {"pe": nc.tensor, "act": nc.scalar, "dve": nc.vector, "pool": nc.gpsimd, "sp": nc.sync}
        self.sem = {}
        self.cnt = {}
        for e in self.eng:
            self.sem[e] = es.enter_context(nc.semaphore("s_" + e))
            self.cnt[e] = 0
        self.waited = {e: {} for e in self.eng}
        self.dq = {}
        for q in ("sp", "pool"):
            sems = [es.enter_context(nc.semaphore("d_%s%d" % (q, i))) for i in range(ndma)]
            self.dq[q] = {"sems": sems, "val": [0] * ndma, "tok": [None] * ndma, "i": 0}
        self.sems_by_key = {}
        for e in self.eng:
            self.sems_by_key[e] = self.sem[e]
        for q in self.dq:
            for i, s in enumerate(self.dq[q]["sems"]):
                self.sems_by_key[(q, i)] = s
        self.nins = 0

    def _wait(self, e, tok):
        if tok is None:
            return
        key, val = tok
        if self.waited[e].get(key, 0) >= val:
            return
        self.waited[e][key] = val
        self.eng[e].wait_ge(self.sems_by_key[key], val)

    def _deps(self, reads, writes):
        toks = []
        for b in reads:
            if b.lw is not None:
                toks.append(b.lw)
        for b in writes:
            if b.lw is not None:
                toks.append(b.lw)
            toks.extend(b.rd)
        return toks

    def _commit(self, tok, reads, writes):
        for b in reads:
            b.rd.append(tok)
        for b in writes:
            b.lw = tok
            b.rd = []

    def op(self, e, fn, reads=(), writes=(), pr=()):
        toks = self._deps(reads, writes)
        for b in pr:
            if b.lw is not None:
                toks.append(b.lw)
            toks.extend(t for t in b.rd if t[0] != e)
        for tok in toks:
            if e == "pe" and tok[0] == "pe":
                continue
            self._wait(e, tok)
        ins = fn(self.eng[e])
        self.cnt[e] += 1
        ins.then_inc(self.sem[e], 1)
        tok = (e, self.cnt[e])
        self._commit(tok, list(reads) + list(pr), writes)
        self.nins += 1
        return tok

    def dma(self, q, out, in_, reads=(), writes=(), **kw):
        d = self.dq[q]
        i = d["i"]
        d["i"] = (i + 1) % len(d["sems"])
        self._wait(q, d["tok"][i])
        for tok in self._deps(reads, writes):
            self._wait(q, tok)
        ins = self.eng[q].dma_start(out=out, in_=in_, **kw)
        d["val"][i] += 16
        ins.then_inc(d["sems"][i], 16)
        tok = ((q, i), d["val"][i])
        d["tok"][i] = tok
        self._commit(tok, reads, writes)
        self.nins += 1
        return tok

    def gather(self, out, in_, idx, reads=(), writes=()):
        q = "pool"
        d = self.dq[q]
        i = d["i"]
        d["i"] = (i + 1) % len(d["sems"])
        self._wait(q, d["tok"][i])
        for tok in self._deps(reads, writes):
            self._wait(q, tok)
        ins = self.nc.gpsimd.indirect_dma_start(out=out, out_offset=None, in_=in_,
                                                in_offset=bass.IndirectOffsetOnAxis(ap=idx, axis=0))
        d["val"][i] += 16
        ins.then_inc(d["sems"][i], 16)
        tok = ((q, i), d["val"][i])
        d["tok"][i] = tok
        self._commit(tok, reads, writes)
        self.nins += 1
        return tok

    def all_tokens(self):
        toks = [(e, self.cnt[e]) for e in self.eng if self.cnt[e] > 0]
        for q, d in self.dq.items():
            toks.extend(t for t in d["tok"] if t is not None)
        return toks

    def barrier(self, engines=None):
        toks = self.all_tokens()
        for e in (engines or self.eng):
            for t in toks:
                if t[0] == e:
                    continue
                self._wait(e, t)


class Rot:
    def __init__(self, tiles):
        self.tiles = tiles
        self.bufs = [Buf() for _ in tiles]
        self.i = 0

    def next(self):
        t, b = self.tiles[self.i], self.bufs[self.i]
        self.i = (self.i + 1) % len(self.tiles)
        return t, b


_UID = [0]


def _uname(name):
    _UID[0] += 1
    return "%s_%d" % (name, _UID[0])


def sb(nc, es, name, shape, dt):
    return es.enter_context(nc.sbuf_tensor(_uname(name), list(shape), dt))


def ps(nc, es, name, shape, dt=F32):
    return es.enter_context(nc.psum_tensor(_uname(name), list(shape), dt))


def rot_sb(nc, es, name, shape, dt, n):
    return Rot([sb(nc, es, "%s%d" % (name, i), shape, dt) for i in range(n)])


def rot_ps(nc, es, name, n):
    return Rot([ps(nc, es, "%s%d" % (name, i), [128, 512], F32) for i in range(n)])


def bcast_rows(ap1, n):
    return ap1.broadcast_to([n, ap1.shape[-1]])


def build(SEQ, debug=False, stop=99):
    NP = 2 * SEQ
    NTOK = NP + NS
    TPC = NP // NCORE
    SPC = NS // NCORE
    NOWN = TPC + SPC
    NGL = SEQ + NS // 2
    assert SEQ % 256 == 0 and TPC % 128 == 0

    nc = bass.Bass("TRN2", target_bir_lowering=False)

    in_names = []

    def din(name, shape, dt=F32, need=0):
        if stop < need:
            return None
        in_names.append(name)
        return nc.dram_tensor(name, list(shape), dt, kind="ExternalInput").ap()

    nc.in_names = in_names

    def dout(name, shape, dt=F32):
        return nc.dram_tensor(name, list(shape), dt, kind="ExternalOutput").ap()

    xT = din("xT", [D, NTOK])
    cs_tab = din("cs_tab", [NTOK, 128])
    w_diff = din("w_diff", [D, 768])
    lam4 = din("lam4", [1, 512])
    dng = din("dng", [1, 256])
    cache_kT = din("cache_kT", [NSB, 2, 128, PAST])
    cache_v = din("cache_v", [NSB, PAST, 256])

    xTg = din("xTg", [D, NGL])
    w_gla = din("w_gla", [D, 1552])
    wg2 = din("wg2", [16, 256])
    bg = din("bg", [1, 256])
    gng = din("gng", [1, 512])
    state_s = din("state_s", [8, 256, 512])
    NT = TPC // 128 + 1
    NPAD = NT * 128
    gidx = din("gidx", [128, NT * 12], mybir.dt.int32)
    x_own = din("x_own", [NPAD, D])
    w_out = din("w_out", [D, D])
    lnp = din("lnp", [4, D])
    peer_wq = din("peer_wq", [D, 2048], need=4.5)
    k1T = din("k1T", [128, 8, 128])
    k2T = din("k2T", [128, 8, 128])
    uT = din("uT", [D, NEXP], need=5.5)
    v_tab = din("v_tab", [NEXP, D], need=5.5)

    k_out = dout("k_out", [NTOK, 256])
    v_out = dout("v_out", [NTOK, 256])
    gla_p = dout("gla_p", [256, 512])
    gla_s = dout("gla_s", [8, 256, 512])
    y_own = dout("y_own", [NPAD, D])
    RAG = NTOK + 2 * NGL
    ag_in = nc.dram_tensor("ag_in", [RAG, 256], BF16).ap()
    ag_out = nc.dram_tensor("ag_out", [NCORE * RAG, 256], BF16).ap()
    h_s = nc.dram_tensor("h_s", [NPAD, D], F32).ap()
    x1_s = nc.dram_tensor("x1_s", [NPAD, D], F32).ap()
    x1T_s = nc.dram_tensor("x1T_s", [D, NPAD], BF16).ap()
    gate_s = nc.dram_tensor("gate_s", [NT, 128, 2056], F32).ap()
    coefT_s = nc.dram_tensor("coefT_s", [NEXP, NPAD], BF16).ap()
    if debug:
        dbg_ag = dout("dbg_ag", [RAG, 256], BF16)
        dbg_x1 = dout("dbg_x1", [NPAD, D])
        dbg_gate = dout("dbg_gate", [NT, 128, 2056])

    qT_s = nc.dram_tensor("qT_s", [2, 128, NTOK], BF16).ap()
    kT_s = nc.dram_tensor("kT_s", [2, 128, NTOK], BF16).ap()
    v_s = nc.dram_tensor("v_s", [NTOK, 256], BF16).ap()

    with ExitStack() as es0:
        S = Sched(nc, es0)
        ident = sb(nc, es0, "ident", [128, 128], BF16)
        b_ident = Buf()
        S.op("pool", lambda e: e.memset(ident[:], 0.0), writes=[b_ident])
        S.op("pool", lambda e: e.affine_select(out=ident[:], in_=ident[:], pattern=[[-1, 128]],
                                                compare_op=ALU.not_equal, fill=1.0, base=0,
                                                channel_multiplier=1), reads=[b_ident], writes=[b_ident])
        b_const = Buf()
        lamt = sb(nc, es0, "lamt", [128, 512], F32)
        lamw = sb(nc, es0, "lamw", [128, 256], F32)
        lams = sb(nc, es0, "lams", [128, 4], F32)
        gd = sb(nc, es0, "gd", [128, 256], F32)
        S.dma("sp", lamt[:], bcast_rows(lam4, 128), writes=[b_const])
        S.dma("sp", gd[:], bcast_rows(dng, 128), writes=[b_const])
        S.op("dve", lambda e: e.tensor_tensor(out=lamw[:, 0:128], in0=lamt[:, 0:128], in1=lamt[:, 128:256], op=ALU.mult),
             reads=[b_const], writes=[b_const])
        S.op("dve", lambda e: e.tensor_tensor(out=lamw[:, 128:256], in0=lamt[:, 256:384], in1=lamt[:, 384:512], op=ALU.mult),
             reads=[b_const], writes=[b_const])
        S.op("dve", lambda e: e.tensor_reduce(out=lams[:, 0:2], in_=lamw[:].rearrange("p (a d) -> p a d", a=2),
                                               axis=AX.X, op=ALU.add), reads=[b_const], writes=[b_const])
        S.op("act", lambda e: e.activation(out=lams[:, 2:4], in_=lams[:, 0:2], func=AF.Exp), reads=[b_const], writes=[b_const])
        S.op("dve", lambda e: e.tensor_tensor(out=lams[:, 0:1], in0=lams[:, 3:4], in1=lams[:, 2:3], op=ALU.subtract),
             reads=[b_const], writes=[b_const])
        S.op("dve", lambda e: e.tensor_scalar(out=lams[:, 3:4], in0=lams[:, 0:1], scalar1=-LAM_INIT, scalar2=None, op0=ALU.add),
             reads=[b_const], writes=[b_const])
        S.op("dve", lambda e: e.tensor_scalar(out=gd[:], in0=gd[:], scalar1=1.0 - LAM_INIT, scalar2=None, op0=ALU.mult),
             reads=[b_const], writes=[b_const])
        neglam = lams[:, 3:4]
        if stop <= 0:
            S.barrier()
            return nc

        with ExitStack() as es:
            wd = sb(nc, es, "wd", [128, 32, 768], BF16)
            b_wd = Buf()
            wsrc = w_diff.rearrange("(kc p) n -> p kc n", p=128)
            for i in range(4):
                S.dma("pool", wd[:, 8 * i:8 * i + 8, :], wsrc[:, 8 * i:8 * i + 8, :], writes=[b_wd])
            xts = rot_sb(nc, es, "xt", [128, 32, 512], BF16, 2)
            css = rot_sb(nc, es, "cs", [128, 4, 128], F32, 2)
            tbs = rot_sb(nc, es, "tb", [128, 4, 512], BF16, 2)
            p_qk = rot_ps(nc, es, "p_qk", 2)
            p_v = rot_ps(nc, es, "p_v", 2)
            p_tr = rot_ps(nc, es, "p_tr", 2)
            rks = rot_sb(nc, es, "rk", [128, 512], F32, 2)
            tmps = rot_sb(nc, es, "rtmp", [128, 512], F32, 2)
            rkbs = rot_sb(nc, es, "rkb", [128, 512], BF16, 2)
            vfs = rot_sb(nc, es, "vf", [128, 256], F32, 2)
            vbs = rot_sb(nc, es, "vb", [128, 256], BF16, 2)
            xsrc = xT.rearrange("(kc p) t -> p kc t", p=128)
            t0 = 0
            while t0 < NTOK:
                n = min(512, NTOK - t0)
                xt, b_xt = xts.next()
                for i in range(4):
                    S.dma("pool", xt[:, 8 * i:8 * i + 8, 0:n], xsrc[:, 8 * i:8 * i + 8, t0:t0 + n], writes=[b_xt])
                cs, b_cs = css.next()
                nj = n // 128
                S.dma("sp", cs[:, 0:nj, :], cs_tab[t0:t0 + n, :].rearrange("(j p) d -> p j d", p=128), writes=[b_cs])
                tb, b_tb = tbs.next()
                for j in range(nj if stop >= 0.6 else 0):
                    tt0 = t0 + j * 128
                    pqk, b_pqk = p_qk.next()
                    pv, b_pv = p_v.next()
                    for kc in range(32):
                        S.op("pe", lambda e, kc=kc: e.matmul(pqk[:], xt[:, kc, j * 128:(j + 1) * 128], wd[:, kc, 0:512],
                                                             start=(kc == 0), stop=(kc == 31)),
                             reads=[b_xt, b_wd], writes=[b_pqk])
                    for kc in range(32):
                        S.op("pe", lambda e, kc=kc: e.matmul(pv[:, 0:256], xt[:, kc, j * 128:(j + 1) * 128], wd[:, kc, 512:768],
                                                             start=(kc == 0), stop=(kc == 31)),
                             reads=[b_xt, b_wd], writes=[b_pv])
                    if stop < 0.7:
                        continue
                    rk, b_rk = rks.next()
                    tmp, b_tmp = tmps.next()
                    q4 = pqk[:].rearrange("p (a h d) -> p a h d", a=4, h=2)
                    r4 = rk[:].rearrange("p (a h d) -> p a h d", a=4, h=2)
                    t4 = tmp[:].rearrange("p (a h d) -> p a h d", a=4, h=2)
                    cosb = cs[:, j, 0:64].unsqueeze(1).broadcast_to([128, 4, 64])
                    sinb = cs[:, j, 64:128].unsqueeze(1).broadcast_to([128, 4, 64])
                    for h in range(2):
                        S.op("dve", lambda e, h=h: e.tensor_tensor(out=r4[:, :, h, :], in0=q4[:, :, h, :], in1=cosb, op=ALU.mult),
                             reads=[b_cs], pr=[b_pqk], writes=[b_rk])
                        S.op("dve", lambda e, h=h: e.tensor_tensor(out=t4[:, :, h, :], in0=q4[:, :, 1 - h, :], in1=sinb, op=ALU.mult),
                             reads=[b_cs], pr=[b_pqk], writes=[b_tmp])
                    S.op("dve", lambda e: e.tensor_tensor(out=r4[:, :, 0, :], in0=r4[:, :, 0, :], in1=t4[:, :, 0, :], op=ALU.subtract),
                         reads=[b_rk, b_tmp], writes=[b_rk])
                    S.op("dve", lambda e: e.tensor_tensor(out=r4[:, :, 1, :], in0=r4[:, :, 1, :], in1=t4[:, :, 1, :], op=ALU.add),
                         reads=[b_rk, b_tmp], writes=[b_rk])
                    if stop < 0.8:
                        continue
                    S.dma("sp", k_out[tt0:tt0 + 128, :], rk[:, 256:512], reads=[b_rk])
                    rkb, b_rkb = rkbs.next()
                    S.op("act", lambda e: e.copy(out=rkb[:], in_=rk[:]), reads=[b_rk], writes=[b_rkb])
                    vf, b_vf = vfs.next()
                    vb, b_vb = vbs.next()
                    S.op("act", lambda e: e.copy(out=vf[:], in_=pv[:, 0:256]), pr=[b_pv], writes=[b_vf])
                    S.op("pool", lambda e: e.tensor_copy(out=vb[:], in_=vf[:]), reads=[b_vf], writes=[b_vb])
                    S.dma("sp", v_out[tt0:tt0 + 128, :], vf[:], reads=[b_vf])
                    S.dma("sp", v_s[tt0:tt0 + 128, :], vb[:], reads=[b_vb])
                    if stop < 0.9:
                        continue
                    ptr, b_ptr = p_tr.next()
                    for a in range(4):
                        S.op("pe", lambda e, a=a: e.matmul(ptr[:, a * 128:(a + 1) * 128], rkb[:, a * 128:(a + 1) * 128], ident[:], start=True, stop=True),
                             reads=[b_rkb, b_ident], writes=[b_ptr])
                    S.op("act", lambda e: e.copy(out=tb[:, :, j * 128:(j + 1) * 128], in_=ptr[:].rearrange("p (a t) -> p a t", a=4)),
                         pr=[b_ptr], writes=[b_tb])
                for a in range(4 if stop >= 0.9 else 0):
                    dst = (qT_s if a < 2 else kT_s)[a % 2, :, t0:t0 + n]
                    S.dma("sp", dst, tb[:, a, 0:n], reads=[b_tb])
                t0 += n
        S.barrier()
        if stop <= 1:
            return nc

        with ExitStack() as es:
            MAXK = max(SEQ, PAST + TS)
            KT = sb(nc, es, "KT", [128, 2, MAXK], BF16)
            QT = sb(nc, es, "QT", [128, 2, SEQ], BF16)
            NB = MAXK // 128 + 1
            V = sb(nc, es, "V", [128, NB, 257], BF16)
            b_KT, b_QT, b_V = Buf(), Buf(), Buf()
            S.op("pool", lambda e: e.memset(V[:], 1.0), writes=[b_V])
            p_s = rot_ps(nc, es, "p_s", 2)
            p_o = [[ps(nc, es, "p_o%d%d" % (a, i), [128, 512]) for i in range(2)] for a in range(2)]
            b_po = [[Buf(), Buf()], [Buf(), Buf()]]
            Es = rot_sb(nc, es, "E", [128, 2, 256], BF16, 3)
            p_t2 = ps(nc, es, "p_t2", [128, 512])
            b_pt2 = Buf()
            fo = rot_sb(nc, es, "fo", [128, 256], F32, 2)
            fsq = rot_sb(nc, es, "fsq", [128, 256], F32, 2)
            fr = rot_sb(nc, es, "fr", [128, 8], F32, 2)
            fob = rot_sb(nc, es, "fob", [128, 256], BF16, 2)
            foT = rot_sb(nc, es, "foT", [128, 2, 128], BF16, 2)

            def finalize(nq, sq, tok0):
                o, b_o = fo.next()
                r, b_r = fr.next()
                sqt, b_sq = fsq.next()
                ob, b_ob = fob.next()
                oT, b_oT = foT.next()
                po0, po1 = p_o[sq][0], p_o[sq][1]
                bo0, bo1 = b_po[sq][0], b_po[sq][1]
                S.op("dve", lambda e: e.reciprocal(out=r[0:nq, 0:1], in_=po0[0:nq, 256:257]), pr=[bo0], writes=[b_r])
                S.op("dve", lambda e: e.reciprocal(out=r[0:nq, 1:2], in_=po1[0:nq, 256:257]), pr=[bo1], writes=[b_r])
                S.op("dve", lambda e: e.tensor_tensor(out=r[0:nq, 2:3], in0=r[0:nq, 1:2], in1=neglam[0:nq, :], op=ALU.mult),
                     reads=[b_r, b_const], writes=[b_r])
                S.op("dve", lambda e: e.tensor_scalar(out=o[0:nq, :], in0=po0[0:nq, 0:256], scalar1=r[0:nq, 0:1], scalar2=None, op0=ALU.mult),
                     reads=[b_r], pr=[bo0], writes=[b_o])
                S.op("dve", lambda e: e.scalar_tensor_tensor(out=o[0:nq, :], in0=po1[0:nq, 0:256], scalar=r[0:nq, 2:3], in1=o[0:nq, :],
                                                             op0=ALU.mult, op1=ALU.add), reads=[b_r, b_o], pr=[bo1], writes=[b_o])
                S.op("dve", lambda e: e.tensor_tensor(out=sqt[0:nq, :], in0=o[0:nq, :], in1=o[0:nq, :], op=ALU.mult),
                     reads=[b_o], writes=[b_sq])
                S.op("dve", lambda e: e.tensor_reduce(out=r[0:nq, 3:4], in_=sqt[0:nq, :], axis=AX.X, op=ALU.add),
                     reads=[b_sq, b_r], writes=[b_r])
                S.op("act", lambda e: e.activation(out=r[0:nq, 4:5], in_=r[0:nq, 3:4], func=AF.Ln, scale=1.0 / 256, bias=eps_t[0:nq, :]),
                     reads=[b_r, b_const], writes=[b_r])
                S.op("act", lambda e: e.activation(out=r[0:nq, 5:6], in_=r[0:nq, 4:5], func=AF.Exp, scale=-0.5),
                     reads=[b_r], writes=[b_r])
                S.op("dve", lambda e: e.scalar_tensor_tensor(out=ob[0:nq, :], in0=o[0:nq, :], scalar=r[0:nq, 5:6], in1=gd[0:nq, :],
                                                             op0=ALU.mult, op1=ALU.mult), reads=[b_o, b_r, b_const], writes=[b_ob])
                S.dma("sp", ag_in[tok0:tok0 + nq, :], ob[0:nq, :], reads=[b_ob])

            eps_t = sb(nc, es, "eps_t", [128, 1], F32)
            S.op("pool", lambda e: e.memset(eps_t[:], EPS), writes=[b_const])

            for b in range(2):
                base = b * SEQ
                for i in range(2):
                    S.dma("sp", KT[:, i, 0:SEQ], kT_s[i, :, base:base + SEQ], writes=[b_KT])
                    S.dma("sp", QT[:, i, 0:SEQ], qT_s[i, :, base:base + SEQ], writes=[b_QT])
                nblk = SEQ // 128
                for c0 in range(0, nblk, 8):
                    c1 = min(nblk, c0 + 8)
                    S.dma("sp", V[:, c0:c1, 0:256],
                          v_s[base + c0 * 128:base + c1 * 128, :].rearrange("(k p) d -> p k d", p=128), writes=[b_V])
                for qg in range(SEQ // 256):
                    q0 = qg * 256
                    last = 2 * qg + 1
                    for kb in range(last + 1):
                        pst_, b_ps = p_s.next()
                        pst = pst_[:].rearrange("p (i q) -> p i q", i=2)
                        E, b_E = Es.next()
                        lo = 128 if kb == last else 0
                        for i in range(2):
                            S.op("pe", lambda e, i=i: e.matmul(pst[:, i, lo:256], KT[:, i, kb * 128:(kb + 1) * 128],
                                                               QT[:, i, q0 + lo:q0 + 256], start=True, stop=True),
                                 reads=[b_KT, b_QT], writes=[b_ps])
                        S.op("act", lambda e: e.activation(out=E[:, :, lo:256], in_=pst[:, :, lo:256], func=AF.Exp, scale=SCALE_D),
                             pr=[b_ps], writes=[b_E])
                        for sq in range(2):
                            if kb == 2 * qg + sq:
                                S.op("pool", lambda e, sq=sq: e.memset(E[64:128, :, sq * 128:sq * 128 + 64], 0.0),
                                     reads=[b_E], writes=[b_E])
                        for sq in range(2):
                            if kb > 2 * qg + sq:
                                continue
                            for i in range(2):
                                S.op("pe", lambda e, i=i, sq=sq: e.matmul(p_o[sq][i][:, 0:257], E[:, i, sq * 128:(sq + 1) * 128],
                                                                        V[:, kb, :], start=(kb == 0), stop=(kb == 2 * qg + sq)),
                                     reads=[b_E, b_V], writes=[b_po[sq][i]])
                    for sq in range(2):
                        finalize(128, sq, base + q0 + sq * 128)

            for s_ in range(NSB):
                tok0 = NP + s_ * TS
                for i in range(2):
                    S.dma("pool", KT[:, i, 0:PAST], cache_kT[s_, i, :, :], writes=[b_KT])
                    S.dma("sp", KT[:, i, PAST:PAST + TS], kT_s[i, :, tok0:tok0 + TS], writes=[b_KT])
                    S.dma("sp", QT[:, i, 0:TS], qT_s[i, :, tok0:tok0 + TS], writes=[b_QT])
                S.dma("pool", V[:, 0:8, 0:256], cache_v[s_].rearrange("(k p) d -> p k d", p=128), writes=[b_V])
                S.dma("sp", V[0:TS, 8, 0:256], v_s[tok0:tok0 + TS, :], writes=[b_V])
                for kb in range(9):
                    nk = 128 if kb < 8 else TS
                    pst_, b_ps = p_s.next()
                    pst = pst_[:].rearrange("p (i q) -> p i q", i=2)
                    E, b_E = Es.next()
                    for i in range(2):
                        S.op("pe", lambda e, i=i: e.matmul(pst[0:nk, i, 0:TS], KT[:, i, kb * 128:kb * 128 + nk],
                                                           QT[:, i, 0:TS], start=True, stop=True),
                             reads=[b_KT, b_QT], writes=[b_ps])
                    S.op("act", lambda e: e.activation(out=E[0:nk, :, 0:TS], in_=pst[0:nk, :, 0:TS], func=AF.Exp, scale=SCALE_D),
                         pr=[b_ps], writes=[b_E])
                    for i in range(2):
                        S.op("pe", lambda e, i=i: e.matmul(p_o[0][i][0:TS, 0:257], E[0:nk, i, 0:TS], V[0:nk, kb, :],
                                                           start=(kb == 0), stop=(kb == 8)),
                             reads=[b_E, b_V], writes=[b_po[0][i]])
                finalize(TS, 0, tok0)
        S.barrier()
        if stop <= 2:
            if debug:
                S.dma("sp", dbg_ag, ag_in)
                S.barrier()
            return nc
        with ExitStack() as es:
            wg = sb(nc, es, "wg", [128, 32, 1552], BF16)
            b_wg = Buf()
            wgsrc = w_gla.rearrange("(kc p) n -> p kc n", p=128)
            for i in range(8):
                S.dma("pool", wg[:, 4 * i:4 * i + 4, :], wgsrc[:, 4 * i:4 * i + 4, :], writes=[b_wg])
            b_gc = Buf()
            wg2b = sb(nc, es, "wg2b", [16, 256], BF16)
            bgt = sb(nc, es, "bgt", [128, 256], F32)
            ggt = sb(nc, es, "ggt", [128, 512], F32)
            S.dma("pool", wg2b[:], wg2, writes=[b_gc])
            S.dma("sp", bgt[:], bcast_rows(bg, 128), writes=[b_gc])
            S.dma("sp", ggt[:], bcast_rows(gng, 128), writes=[b_gc])
            eps_g = sb(nc, es, "eps_g", [128, 1], F32)
            S.op("pool", lambda e: e.memset(eps_g[:], EPS), writes=[b_gc])

            def make_masks(L, tag):
                nch = 128 // L
                tri = sb(nc, es, "tri" + tag, [128, 128], F32)
                dm = sb(nc, es, "dm" + tag, [128, 128], F32)
                rm = sb(nc, es, "rm" + tag, [128, nch], F32)
                S.op("pool", lambda e: e.memset(tri[:], 1.0), writes=[b_gc])
                S.op("pool", lambda e: e.memset(dm[:], 1.0), writes=[b_gc])
                S.op("pool", lambda e: e.memset(rm[:], 1.0), writes=[b_gc])
                t3 = tri[:].rearrange("p (c l) -> p c l", c=nch)
                d3 = dm[:].rearrange("p (c l) -> p c l", c=nch)
                S.op("pool", lambda e: e.affine_select(out=tri[:], in_=tri[:], pattern=[[1, 128]], compare_op=ALU.is_ge,
                                                        fill=0.0, base=0, channel_multiplier=-1), reads=[b_gc], writes=[b_gc])
                S.op("pool", lambda e: e.affine_select(out=t3, in_=t3, pattern=[[-L, nch], [0, L]], compare_op=ALU.is_ge,
                                                        fill=0.0, base=0, channel_multiplier=1), reads=[b_gc], writes=[b_gc])
                S.op("pool", lambda e: e.affine_select(out=dm[:], in_=dm[:], pattern=[[-1, 128]], compare_op=ALU.is_ge,
                                                        fill=0.0, base=-1, channel_multiplier=1), reads=[b_gc], writes=[b_gc])
                S.op("pool", lambda e: e.affine_select(out=d3, in_=d3, pattern=[[L, nch], [0, L]], compare_op=ALU.is_ge,
                                                        fill=0.0, base=L - 1, channel_multiplier=-1), reads=[b_gc], writes=[b_gc])
                S.op("pool", lambda e: e.affine_select(out=rm[:], in_=rm[:], pattern=[[-L, nch]], compare_op=ALU.is_ge,
                                                        fill=0.0, base=0, channel_multiplier=1), reads=[b_gc], writes=[b_gc])
                S.op("pool", lambda e: e.affine_select(out=rm[:], in_=rm[:], pattern=[[L, nch]], compare_op=ALU.is_ge,
                                                        fill=0.0, base=L - 1, channel_multiplier=-1), reads=[b_gc], writes=[b_gc])
                qtm = []
                for j in range(nch):
                    t = sb(nc, es, "qtm%s_%d" % (tag, j), [128, 2, 128], BF16)
                    S.op("pool", lambda e, t=t: e.memset(t[:], 0.0), writes=[b_gc])
                    qtm.append((t, Buf()))
                return tri, dm, rm, qtm

            masks = {64: make_masks(64, "a"), 16: make_masks(16, "b")}

            xgs = rot_sb(nc, es, "xg", [128, 32, 256], BF16, 2)
            pA = ps(nc, es, "pA", [128, 512]); pB = ps(nc, es, "pB", [128, 512]); pC = ps(nc, es, "pC", [128, 512])
            pD = ps(nc, es, "pD", [128, 512]); pE = ps(nc, es, "pE", [128, 512]); pF = ps(nc, es, "pF", [128, 512])
            pG = ps(nc, es, "pG", [128, 512]); pO = ps(nc, es, "pO", [128, 512])
            b_pA, b_pB, b_pC, b_pD, b_pE, b_pF, b_pG, b_pO = [Buf() for _ in range(8)]
            gzb = sb(nc, es, "gzb", [128, 16], BF16); b_gzb = Buf()
            gzT = sb(nc, es, "gzT", [16, 128], BF16); b_gzT = Buf()
            vbs = rot_sb(nc, es, "gvb", [128, 512], BF16, 2)
            sgs = rot_sb(nc, es, "gsg", [128, 512], F32, 2)
            zt = sb(nc, es, "zt", [128, 256], F32); b_zt = Buf()
            lt_ = sb(nc, es, "lt_", [128, 256], F32); b_lt = Buf()
            ebt = sb(nc, es, "ebt", [128, 256], F32); b_eb = Buf()
            enbt = sb(nc, es, "enbt", [128, 256], F32); b_enb = Buf()
            edt = sb(nc, es, "edt", [128, 256], F32); b_ed = Buf()
            eblt = sb(nc, es, "eblt", [128, 2, 8], F32); b_ebl = Buf()
            qin = sb(nc, es, "qin", [128, 256], BF16); b_qin = Buf()
            kin = sb(nc, es, "kin", [128, 256], BF16); b_kin = Buf()
            kend = sb(nc, es, "kend", [128, 256], F32); b_kend = Buf()
            kms = rot_sb(nc, es, "km", [128, 256], BF16, 2)
            qTf = sb(nc, es, "qTf", [128, 2, 128], BF16); b_qTf = Buf()
            kTf = sb(nc, es, "kTf", [128, 2, 128], BF16); b_kTf = Buf()
            aTm = sb(nc, es, "aTm", [128, 128], BF16); b_aTm = Buf()
            Sfs = rot_sb(nc, es, "Sf", [128, 2, 512], F32, 2)
            Sbs = rot_sb(nc, es, "Sb", [128, 2, 512], BF16, 2)
            of_ = sb(nc, es, "of_", [128, 512], F32); b_of = Buf()
            osq = sb(nc, es, "osq", [128, 512], F32); b_osq = Buf()
            orr = sb(nc, es, "orr", [128, 4], F32); b_orr = Buf()
            obs = rot_sb(nc, es, "gob", [128, 512], BF16, 2)
            xgsrc = xTg.rearrange("(kc p) t -> p kc t", p=128)
            ag_g = ag_in[NTOK:RAG, :].rearrange("(t two) d -> t (two d)", two=2)

            def gla_tile(xt, b_xt, col0, L, ltok0, state):
                nch = 128 // L
                tri, dm, rm, qtm = masks[L]
                xs = lambda kc: xt[:, kc, col0:col0 + 128]
                for kc in range(32):
                    st, sp_ = (kc == 0), (kc == 31)
                    S.op("pe", lambda e, kc=kc: e.matmul(pA[:], xs(kc), wg[:, kc, 0:512], start=st, stop=sp_), reads=[b_xt, b_wg], writes=[b_pA])
                    S.op("pe", lambda e, kc=kc: e.matmul(pB[:], xs(kc), wg[:, kc, 512:1024], start=st, stop=sp_), reads=[b_xt, b_wg], writes=[b_pB])
                    S.op("pe", lambda e, kc=kc: e.matmul(pC[:], xs(kc), wg[:, kc, 1024:1536], start=st, stop=sp_), reads=[b_xt, b_wg], writes=[b_pC])
                    S.op("pe", lambda e, kc=kc: e.matmul(pD[:, 0:16], xs(kc), wg[:, kc, 1536:1552], start=st, stop=sp_), reads=[b_xt, b_wg], writes=[b_pD])
                vb, b_vb = vbs.next()
                sg, b_sg = sgs.next()
                S.op("act", lambda e: e.copy(out=gzb[:], in_=pD[:, 0:16]), pr=[b_pD], writes=[b_gzb])
                S.op("act", lambda e: e.copy(out=vb[:], in_=pB[:]), pr=[b_pB], writes=[b_vb])
                S.op("pe", lambda e: e.matmul(pD[0:16, 128:256], gzb[:], ident[:], start=True, stop=True), reads=[b_gzb, b_ident], writes=[b_pD])
                S.op("dve", lambda e: e.tensor_copy(out=gzT[:], in_=pD[0:16, 128:256]), pr=[b_pD], writes=[b_gzT])
                S.op("pe", lambda e: e.matmul(pE[:, 0:256], gzT[:], wg2b[:], start=True, stop=True), reads=[b_gzT, b_gc], writes=[b_pE])
                S.op("dve", lambda e: e.tensor_tensor(out=zt[:], in0=pE[:, 0:256], in1=bgt[:], op=ALU.add), reads=[b_gc], pr=[b_pE], writes=[b_zt])
                S.op("act", lambda e: e.activation(out=lt_[:], in_=zt[:], func=AF.Exp, scale=-1.0), reads=[b_zt], writes=[b_lt])
                S.op("act", lambda e: e.activation(out=lt_[:], in_=lt_[:], func=AF.Ln, bias=1.0, scale=1.0), reads=[b_lt], writes=[b_lt])
                S.op("pe", lambda e: e.matmul(pE[:, 256:512], tri[:], lt_[:], start=True, stop=True), reads=[b_lt, b_gc], writes=[b_pE])
                S.op("pe", lambda e: e.matmul(pF[:, 0:256], dm[:], lt_[:], start=True, stop=True), reads=[b_lt, b_gc], writes=[b_pF])
                for hh in range(2):
                    S.op("pe", lambda e, hh=hh: e.matmul(pF[:, 256 + 8 * hh:256 + 8 * hh + nch], lt_[:, hh * 128:(hh + 1) * 128], rm[:],
                                                       start=True, stop=True), reads=[b_lt, b_gc], writes=[b_pF])
                S.op("act", lambda e: e.activation(out=ebt[:], in_=pE[:, 256:512], func=AF.Exp, scale=-1.0 / 16), pr=[b_pE], writes=[b_eb])
                S.op("act", lambda e: e.activation(out=enbt[:], in_=pE[:, 256:512], func=AF.Exp, scale=1.0 / 16), pr=[b_pE], writes=[b_enb])
                S.op("act", lambda e: e.activation(out=edt[:], in_=pF[:, 0:256], func=AF.Exp, scale=-1.0 / 16), pr=[b_pF], writes=[b_ed])
                S.op("act", lambda e: e.activation(out=eblt[:, :, 0:nch], in_=pF[:, 256:272].rearrange("p (h c) -> p h c", h=2)[:, :, 0:nch],
                                                   func=AF.Exp, scale=-1.0 / 16), pr=[b_pF], writes=[b_ebl])
                S.op("act", lambda e: e.activation(out=sg[:], in_=pC[:], func=AF.Silu), pr=[b_pC], writes=[b_sg])
                S.op("dve", lambda e: e.scalar_tensor_tensor(out=qin[:], in0=pA[:, 0:256], scalar=SCALE_G, in1=ebt[:], op0=ALU.mult, op1=ALU.mult),
                     reads=[b_eb], pr=[b_pA], writes=[b_qin])
                S.op("dve", lambda e: e.tensor_tensor(out=kin[:], in0=pA[:, 256:512], in1=enbt[:], op=ALU.mult), reads=[b_enb], pr=[b_pA], writes=[b_kin])
                S.op("dve", lambda e: e.tensor_tensor(out=kend[:], in0=pA[:, 256:512], in1=edt[:], op=ALU.mult), reads=[b_ed], pr=[b_pA], writes=[b_kend])
                for hh in range(2):
                    S.op("pe", lambda e, hh=hh: e.matmul(pG[:, hh * 128:(hh + 1) * 128], qin[:, hh * 128:(hh + 1) * 128], ident[:], start=True, stop=True),
                         reads=[b_qin, b_ident], writes=[b_pG])
                    S.op("pe", lambda e, hh=hh: e.matmul(pG[:, 256 + hh * 128:256 + (hh + 1) * 128], kin[:, hh * 128:(hh + 1) * 128], ident[:], start=True, stop=True),
                         reads=[b_kin, b_ident], writes=[b_pG])
                g4 = pG[:].rearrange("p (a h t) -> p a h t", a=2, h=2)
                S.op("dve", lambda e: e.tensor_copy(out=qTf[:], in_=g4[:, 0, :, :]), pr=[b_pG], writes=[b_qTf])
                S.op("dve", lambda e: e.tensor_copy(out=kTf[:], in_=g4[:, 1, :, :]), pr=[b_pG], writes=[b_kTf])
                for j in range(nch):
                    t, b_t = qtm[j]
                    S.op("dve", lambda e, t=t, j=j: e.tensor_copy(out=t[:, :, j * L:(j + 1) * L], in_=g4[:, 0, :, j * L:(j + 1) * L]), pr=[b_pG], writes=[b_t])
                for hh in range(2):
                    S.op("pe", lambda e, hh=hh: e.matmul(pD[:, 256:384], kTf[:, hh, :], qTf[:, hh, :], start=(hh == 0), stop=(hh == 1)),
                         reads=[b_kTf, b_qTf], writes=[b_pD])
                S.op("dve", lambda e: e.tensor_tensor(out=aTm[:], in0=pD[:, 256:384], in1=tri[:], op=ALU.mult), reads=[b_gc], pr=[b_pD], writes=[b_aTm])
                S.op("pe", lambda e: e.matmul(pO[:], aTm[:], vb[:], start=True, stop=False), reads=[b_aTm, b_vb], writes=[b_pO])
                for j in range(nch):
                    if state["carry"]:
                        Sf, b_Sf, Sb, b_Sb = state["S"]
                    else:
                        Sf, b_Sf = Sfs.next()
                        Sb, b_Sb = Sbs.next()
                        S.dma("sp", Sf[:], state_s[j].rearrange("(h p) v -> p h v", p=128), writes=[b_Sf])
                        S.op("act", lambda e, Sf=Sf, Sb=Sb: e.copy(out=Sb[:], in_=Sf[:]), reads=[b_Sf], writes=[b_Sb])
                    t, b_t = qtm[j]
                    for hh in range(2):
                        S.op("pe", lambda e, hh=hh, t=t, Sb=Sb: e.matmul(pO[:], t[:, hh, :], Sb[:, hh, :], start=False, stop=(j == nch - 1 and hh == 1)),
                             reads=[b_t, b_Sb], writes=[b_pO])
                    km, b_km = kms.next()
                    S.op("dve", lambda e, km=km, j=j: e.tensor_scalar(out=km[:], in0=kend[:], scalar1=rm[:, j:j + 1], scalar2=None, op0=ALU.mult),
                         reads=[b_kend, b_gc], writes=[b_km])
                    for hh, (pp, b_pp) in enumerate(((pA, b_pA), (pB, b_pB))):
                        S.op("pe", lambda e, hh=hh, pp=pp, km=km: e.matmul(pp[:], km[:, hh * 128:(hh + 1) * 128], vb[:], start=True, stop=True),
                             reads=[b_km, b_vb], writes=[b_pp])
                        S.op("dve", lambda e, hh=hh, pp=pp, Sf=Sf, j=j: e.scalar_tensor_tensor(out=Sf[:, hh, :], in0=Sf[:, hh, :], scalar=eblt[:, hh, j:j + 1],
                                                                                          in1=pp[:], op0=ALU.mult, op1=ALU.add),
                             reads=[b_ebl], pr=[b_pp], writes=[b_Sf])
                    if state["carry"]:
                        S.op("act", lambda e, Sf=Sf, Sb=Sb: e.copy(out=Sb[:], in_=Sf[:]), reads=[b_Sf], writes=[b_Sb])
                    else:
                        S.dma("sp", gla_s[j].rearrange("(h p) v -> p h v", p=128), Sf[:], reads=[b_Sf])
                ob, b_ob = obs.next()
                S.op("act", lambda e: e.copy(out=of_[:], in_=pO[:]), pr=[b_pO], writes=[b_of])
                S.op("dve", lambda e: e.tensor_tensor(out=osq[:], in0=of_[:], in1=of_[:], op=ALU.mult), reads=[b_of], writes=[b_osq])
                S.op("dve", lambda e: e.tensor_reduce(out=orr[:, 0:1], in_=osq[:], axis=AX.X, op=ALU.add), reads=[b_osq], writes=[b_orr])
                S.op("act", lambda e: e.activation(out=orr[:, 1:2], in_=orr[:, 0:1], func=AF.Ln, scale=1.0 / 512, bias=eps_g[:]), reads=[b_orr, b_gc], writes=[b_orr])
                S.op("act", lambda e: e.activation(out=orr[:, 2:3], in_=orr[:, 1:2], func=AF.Exp, scale=-0.5), reads=[b_orr], writes=[b_orr])
                S.op("dve", lambda e: e.scalar_tensor_tensor(out=of_[:], in0=of_[:], scalar=orr[:, 2:3], in1=ggt[:], op0=ALU.mult, op1=ALU.mult),
                     reads=[b_orr, b_gc, b_osq], writes=[b_of])
                S.op("dve", lambda e: e.tensor_tensor(out=ob[:], in0=of_[:], in1=sg[:], op=ALU.mult), reads=[b_of, b_sg], writes=[b_ob])
                S.dma("sp", ag_g[ltok0:ltok0 + 128, :], ob[:], reads=[b_ob])

            Sf0, b_Sf0 = Sfs.next()
            Sb0, b_Sb0 = Sbs.next()
            S.op("pool", lambda e: e.memset(Sf0[:], 0.0), writes=[b_Sf0])
            S.op("pool", lambda e: e.memset(Sb0[:], 0.0), writes=[b_Sb0])
            pstate = {"carry": True, "S": (Sf0, b_Sf0, Sb0, b_Sb0)}
            for t0 in range(0, SEQ, 256):
                xt, b_xt = xgs.next()
                for i in range(4):
                    S.dma("pool", xt[:, 8 * i:8 * i + 8, :], xgsrc[:, 8 * i:8 * i + 8, t0:t0 + 256], writes=[b_xt])
                for j in range(2):
                    gla_tile(xt, b_xt, j * 128, 64, t0 + j * 128, pstate)
            S.dma("sp", gla_p.rearrange("(h p) v -> p h v", p=128), Sf0[:], reads=[b_Sf0])
            xt, b_xt = xgs.next()
            for i in range(4):
                S.dma("pool", xt[:, 8 * i:8 * i + 8, 0:128], xgsrc[:, 8 * i:8 * i + 8, SEQ:SEQ + 128], writes=[b_xt])
            S.barrier()
            gla_tile(xt, b_xt, 0, 16, SEQ, {"carry": False})
        S.barrier()
        if stop <= 3:
            if debug:
                S.dma("sp", dbg_ag, ag_in)
                S.barrier()
            return nc
        ccsem = es0.enter_context(nc.semaphore("ccsem"))
        nc.gpsimd.collective_compute("AllGather", ALU.bypass, replica_groups=[list(range(NCORE))],
                                     ins=[ag_in.opt()], outs=[ag_out.opt()]).then_inc(ccsem, 1)
        nc.gpsimd.wait_ge(ccsem, 1)
        ag512 = ag_out.rearrange("(r two) d -> r (two d)", two=2)
        I32 = mybir.dt.int32
        git = sb(nc, es0, "git", [128, NT * 12], I32)
        b_git = Buf()
        S.dma("sp", git[:], gidx, writes=[b_git])
        epsc = sb(nc, es0, "epsc", [128, 1], F32)
        S.op("pool", lambda e: e.memset(epsc[:], EPS), writes=[b_const])

        def transpose_rows(src, b_src, dst_fn, b_dst, ptrs):
            for q in range(8):
                ptr, b_ptr = ptrs.next()
                for a in range(4):
                    fc = q * 4 + a
                    S.op("pe", lambda e: e.matmul(ptr[:, a * 128:(a + 1) * 128], src[:, fc * 128:(fc + 1) * 128], ident[:], start=True, stop=True),
                         reads=[b_src, b_ident], writes=[b_ptr])
                eng = "act" if q % 2 == 0 else "dve"
                if eng == "act":
                    S.op("act", lambda e: e.copy(out=dst_fn(q), in_=ptr[:].rearrange("p (a t) -> p a t", a=4)), pr=[b_ptr], writes=[b_dst])
                else:
                    S.op("dve", lambda e: e.tensor_copy(out=dst_fn(q), in_=ptr[:].rearrange("p (a t) -> p a t", a=4)), pr=[b_ptr], writes=[b_dst])

        def ln_pass(grow, brow, to_x1):
            with ExitStack() as es:
                gt_ = sb(nc, es, "ln_g", [128, D], F32)
                bt_ = sb(nc, es, "ln_b", [128, D], F32)
                b_gb = Buf()
                S.dma("sp", gt_[:], bcast_rows(lnp[grow:grow + 1, :], 128), writes=[b_gb])
                S.dma("sp", bt_[:], bcast_rows(lnp[brow:brow + 1, :], 128), writes=[b_gb])
                hs = rot_sb(nc, es, "ln_h", [128, D], F32, 2)
                junk = sb(nc, es, "ln_junk", [128, D], BF16); b_junk = Buf()
                st = rot_sb(nc, es, "ln_st", [128, 8], F32, 2)
                if to_x1:
                    xbs = rot_sb(nc, es, "ln_xb", [128, D], BF16, 2)
                    xTs = rot_sb(nc, es, "ln_xT", [128, 32, 128], BF16, 2)
                    ptrs = rot_ps(nc, es, "ln_ptr", 4)
                for tt in range(NT):
                    h, b_h = hs.next()
                    s_, b_s = st.next()
                    S.dma("sp", h[:], h_s[tt * 128:(tt + 1) * 128, :], writes=[b_h])
                    S.op("dve", lambda e: e.tensor_reduce(out=s_[:, 0:1], in_=h[:], axis=AX.X, op=ALU.add), reads=[b_h], writes=[b_s])
                    S.op("dve", lambda e: e.scalar_tensor_tensor(out=junk[:], in0=h[:], scalar=1.0, in1=h[:], op0=ALU.mult, op1=ALU.mult,
                                                                 accum_out=s_[:, 1:2]), reads=[b_h, b_s], writes=[b_junk, b_s])
                    S.op("dve", lambda e: e.tensor_scalar(out=s_[:, 2:3], in0=s_[:, 0:1], scalar1=-1.0 / D, scalar2=None, op0=ALU.mult), reads=[b_s], writes=[b_s])
                    S.op("dve", lambda e: e.tensor_tensor(out=s_[:, 3:4], in0=s_[:, 2:3], in1=s_[:, 2:3], op=ALU.mult), reads=[b_s], writes=[b_s])
                    S.op("dve", lambda e: e.scalar_tensor_tensor(out=s_[:, 4:5], in0=s_[:, 1:2], scalar=1.0 / D, in1=s_[:, 3:4], op0=ALU.mult, op1=ALU.subtract),
                         reads=[b_s], writes=[b_s])
                    S.op("act", lambda e: e.activation(out=s_[:, 5:6], in_=s_[:, 4:5], func=AF.Ln, bias=epsc[:], scale=1.0), reads=[b_s, b_const], writes=[b_s])
                    S.op("act", lambda e: e.activation(out=s_[:, 6:7], in_=s_[:, 5:6], func=AF.Exp, scale=-0.5), reads=[b_s], writes=[b_s])
                    S.op("dve", lambda e: e.tensor_scalar(out=h[:], in0=h[:], scalar1=s_[:, 2:3], scalar2=s_[:, 6:7], op0=ALU.add, op1=ALU.mult),
                         reads=[b_s, b_junk], writes=[b_h])
                    S.op("pool", lambda e: e.tensor_tensor(out=h[:], in0=h[:], in1=gt_[:], op=ALU.mult), reads=[b_gb], writes=[b_h])
                    S.op("dve", lambda e: e.tensor_tensor(out=h[:], in0=h[:], in1=bt_[:], op=ALU.add), reads=[b_gb], writes=[b_h])
                    if to_x1:
                        S.dma("sp", x1_s[tt * 128:(tt + 1) * 128, :], h[:], reads=[b_h])
                        xb, b_xb = xbs.next()
                        xT_, b_xT = xTs.next()
                        S.op("act", lambda e: e.copy(out=xb[:], in_=h[:]), reads=[b_h], writes=[b_xb])
                        transpose_rows(xb, b_xb, lambda q: xT_[:, q * 4:(q + 1) * 4, :], b_xT, ptrs)
                        S.dma("sp", x1T_s[:, tt * 128:(tt + 1) * 128].rearrange("(fc p) t -> p fc t", p=128), xT_[:], reads=[b_xT])
                    else:
                        S.dma("sp", y_own[tt * 128:(tt + 1) * 128, :], h[:], reads=[b_h])

        NH = (NT + 1) // 2
        for half in range(2):
            tiles = list(range(half * NH, min(NT, (half + 1) * NH)))
            with ExitStack() as es:
                mixT = sb(nc, es, "mixT", [128, NH, 32, 128], BF16)
                b_mixT = [Buf() for _ in range(NH)]
                mts = rot_sb(nc, es, "mt", [128, D], BF16, 2)
                ptrs = rot_ps(nc, es, "p4tr", 3)
                pws = rot_ps(nc, es, "p4w", 3)
                wsl = rot_sb(nc, es, "wsl", [128, 32, 256], BF16, 2)
                xos = rot_sb(nc, es, "xo", [128, 256], F32, 3)
                hts = rot_sb(nc, es, "ht", [128, 256], F32, 3)
                for li, tt in enumerate(tiles):
                    mt, b_mt = mts.next()
                    for hh in range(8):
                        S.gather(mt[:, hh * 256:(hh + 1) * 256], ag_out, git[:, tt * 12 + hh:tt * 12 + hh + 1], reads=[b_git], writes=[b_mt])
                    for gg in range(4):
                        S.gather(mt[:, 2048 + gg * 512:2048 + (gg + 1) * 512], ag512, git[:, tt * 12 + 8 + gg:tt * 12 + 9 + gg],
                                 reads=[b_git], writes=[b_mt])
                    transpose_rows(mt, b_mt, lambda q, li=li: mixT[:, li, q * 4:(q + 1) * 4, :], b_mixT[li], ptrs)
                wosrc = w_out.rearrange("(fc p) n -> p fc n", p=128)
                for ds_ in range(D // 256):
                    ws, b_ws = wsl.next()
                    for i in range(4):
                        S.dma("pool", ws[:, 8 * i:8 * i + 8, :], wosrc[:, 8 * i:8 * i + 8, ds_ * 256:(ds_ + 1) * 256], writes=[b_ws])
                    for li, tt in enumerate(tiles):
                        pw, b_pw = pws.next()
                        xo, b_xo = xos.next()
                        ht, b_ht = hts.next()
                        S.dma("sp", xo[:], x_own[tt * 128:(tt + 1) * 128, ds_ * 256:(ds_ + 1) * 256], writes=[b_xo])
                        for fc in range(32):
                            S.op("pe", lambda e: e.matmul(pw[:, 0:256], mixT[:, li, fc, :], ws[:, fc, :], start=(fc == 0), stop=(fc == 31)),
                                 reads=[b_mixT[li], b_ws], writes=[b_pw])
                        S.op("dve", lambda e: e.scalar_tensor_tensor(out=ht[:], in0=xo[:], scalar=ALPHA, in1=pw[:, 0:256], op0=ALU.mult, op1=ALU.add),
                             reads=[b_xo], pr=[b_pw], writes=[b_ht])
                        S.dma("sp", h_s[tt * 128:(tt + 1) * 128, ds_ * 256:(ds_ + 1) * 256], ht[:], reads=[b_ht])
            S.barrier()
        ln_pass(0, 1, True)
        S.barrier()
        if stop <= 4:
            if debug:
                S.dma("sp", dbg_x1, x1_s)
                S.barrier()
            return nc

        groups = [list(range(g0, min(NT, g0 + 4))) for g0 in range(0, NT, 4)]
        x1Tv = x1T_s.rearrange("(fc p) t -> p fc t", p=128)

        with ExitStack() as es:
            k1b = sb(nc, es, "k1b", [128, 8, 128], BF16)
            k2b = sb(nc, es, "k2b", [128, 8, 128], BF16)
            b_kb = Buf()
            S.dma("pool", k1b[:], k1T, writes=[b_kb])
            S.dma("pool", k2b[:], k2T, writes=[b_kb])
            xgs5 = rot_sb(nc, es, "xg5", [128, 32, 512], BF16, 2)
            wqs = rot_sb(nc, es, "wqb", [128, 32, 128], BF16, 2)
            qT = sb(nc, es, "qT", [128, 16, 512], BF16); b_qT = Buf()
            pqs = rot_ps(nc, es, "p5q", 2)
            psc = [ps(nc, es, "p5s%d" % i, [128, 512]) for i in range(4)]
            b_psc = [Buf() for _ in range(4)]
            s1t = sb(nc, es, "s1t", [128, 8, 128], F32); b_s1 = Buf()
            gts = rot_sb(nc, es, "gt5", [128, 2056], F32, 2)
            t1 = sb(nc, es, "t1", [128, 8, 16], F32); b_t1 = Buf()
            t2 = sb(nc, es, "t2", [128, 8, 16], F32); b_t2 = Buf()
            tmp = sb(nc, es, "tk_tmp", [128, 128], F32); b_tmp = Buf()
            cand = sb(nc, es, "cand", [128, 16, 16], F32); b_cand = Buf()
            tmp2 = sb(nc, es, "tk_tmp2", [128, 256], F32); b_tmp2 = Buf()
            Bt = sb(nc, es, "Bt", [128, 8, 16], F32); b_B = Buf()
            Bm = sb(nc, es, "Bm", [128, 8, 16], F32); b_Bm = Buf()
            sm = sb(nc, es, "sm", [128, 4, 8], F32); b_sm = Buf()
            wqsrc = peer_wq.rearrange("(fc p) n -> p fc n", p=128)

            def top16(src2d, dst16, b_src, b_dst, scratch, b_scr):
                S.op("dve", lambda e: e.max(out=dst16[:, 0:8], in_=src2d), reads=[b_src], writes=[b_dst])
                S.op("dve", lambda e: e.match_replace(out=scratch, in_to_replace=dst16[:, 0:8], in_values=src2d, imm_value=-1e30),
                     reads=[b_src, b_dst], writes=[b_scr])
                S.op("dve", lambda e: e.max(out=dst16[:, 8:16], in_=scratch), reads=[b_scr], writes=[b_dst])

            for grp in groups:
                N = 128 * len(grp)
                c0 = grp[0] * 128
                xg, b_xg = xgs5.next()
                for i in range(4):
                    S.dma("sp", xg[:, 8 * i:8 * i + 8, 0:N], x1Tv[:, 8 * i:8 * i + 8, c0:c0 + N], writes=[b_xg])
                for blk in range(16):
                    wqb, b_wq = wqs.next()
                    S.dma("pool", wqb[:], wqsrc[:, :, blk * 128:(blk + 1) * 128], writes=[b_wq])
                    pq, b_pq = pqs.next()
                    for fc in range(32):
                        S.op("pe", lambda e: e.matmul(pq[:, 0:N], wqb[:, fc, :], xg[:, fc, 0:N], start=(fc == 0), stop=(fc == 31)),
                             reads=[b_wq, b_xg], writes=[b_pq])
                    S.op("act", lambda e: e.copy(out=qT[:, blk, 0:N], in_=pq[:, 0:N]), pr=[b_pq], writes=[b_qT])
                for li, tt in enumerate(grp):
                    gt, b_gt = gts.next()
                    tk = slice(li * 128, (li + 1) * 128)
                    for hh in range(8):
                        S.op("pe", lambda e: e.matmul(psc[hh // 4][:, (hh % 4) * 128:(hh % 4 + 1) * 128], qT[:, 2 * hh, tk], k1b[:, hh, :], start=True, stop=True),
                             reads=[b_qT, b_kb], writes=[b_psc[hh // 4]])
                        S.op("pe", lambda e: e.matmul(psc[2 + hh // 4][:, (hh % 4) * 128:(hh % 4 + 1) * 128], qT[:, 2 * hh + 1, tk], k2b[:, hh, :], start=True, stop=True),
                             reads=[b_qT, b_kb], writes=[b_psc[2 + hh // 4]])
                    s2v = gt[:, 1024:2048].rearrange("p (h k) -> p h k", h=8)
                    for i in range(2):
                        S.op("act", lambda e: e.copy(out=s1t[:, 4 * i:4 * i + 4, :], in_=psc[i][:].rearrange("p (h k) -> p h k", h=4)), pr=[b_psc[i]], writes=[b_s1])
                        S.op("act", lambda e: e.copy(out=s2v[:, 4 * i:4 * i + 4, :], in_=psc[2 + i][:].rearrange("p (h k) -> p h k", h=4)), pr=[b_psc[2 + i]], writes=[b_gt])
                    for hh in range(8):
                        top16(s1t[:, hh, :], t1[:, hh, :], b_s1, b_t1, tmp[:], b_tmp)
                        top16(s2v[:, hh, :], t2[:, hh, :], b_gt, b_t2, tmp[:], b_tmp)
                        S.op("dve", lambda e: e.tensor_tensor(out=cand[:], in0=t1[:, hh, :].unsqueeze(2).broadcast_to([128, 16, 16]),
                                                               in1=t2[:, hh, :].unsqueeze(1).broadcast_to([128, 16, 16]), op=ALU.add),
                             reads=[b_t1, b_t2], writes=[b_cand])
                        top16(cand[:].rearrange("p a b -> p (a b)"), Bt[:, hh, :], b_cand, b_B, tmp2[:], b_tmp2)
                    S.op("dve", lambda e: e.tensor_tensor(out=Bm[:], in0=Bt[:], in1=Bt[:, :, 0:1].broadcast_to([128, 8, 16]), op=ALU.subtract),
                         reads=[b_B], writes=[b_Bm])
                    S.op("act", lambda e: e.activation(out=Bm[:], in_=Bm[:], func=AF.Exp), reads=[b_Bm], writes=[b_Bm])
                    S.op("dve", lambda e: e.tensor_reduce(out=sm[:, 0, :], in_=Bm[:], axis=AX.X, op=ALU.add), reads=[b_Bm], writes=[b_sm])
                    S.op("act", lambda e: e.activation(out=sm[:, 1, :], in_=sm[:, 0, :], func=AF.Ln), reads=[b_sm], writes=[b_sm])
                    S.op("dve", lambda e: e.tensor_tensor(out=sm[:, 2, :], in0=Bt[:, :, 15], in1=Bt[:, :, 0], op=ALU.subtract), reads=[b_B, b_sm], writes=[b_sm])
                    S.op("dve", lambda e: e.tensor_tensor(out=gt[:, 2048:2056], in0=sm[:, 2, :], in1=sm[:, 1, :], op=ALU.subtract), reads=[b_sm], writes=[b_gt])
                    S.op("dve", lambda e: e.tensor_scalar(out=sm[:, 3, :], in0=Bt[:, :, 15], scalar1=-1e-5, scalar2=None, op0=ALU.add), reads=[b_B, b_sm], writes=[b_sm])
                    S.op("dve", lambda e: e.tensor_tensor(out=gt[:, 0:1024].rearrange("p (h k) -> p h k", h=8), in0=s1t[:],
                                                           in1=sm[:, 3, :].unsqueeze(2).broadcast_to([128, 8, 128]), op=ALU.subtract),
                         reads=[b_s1, b_sm], writes=[b_gt])
                    S.dma("sp", gate_s[tt], gt[:], reads=[b_gt])
        S.barrier()
        if stop <= 5:
            if debug:
                S.dma("sp", dbg_gate, gate_s)
                S.barrier()
            return nc

        with ExitStack() as es:
            xg6 = sb(nc, es, "xg6", [128, 32, 512], BF16); b_xg6 = Buf()
            gate6 = sb(nc, es, "gate6", [128, 4, 2056], F32); b_gate6 = Buf()
            uts = rot_sb(nc, es, "ut", [128, 32, 256], BF16, 2)
            spps = rot_sb(nc, es, "spp", [128, 8, 2, 128], F32, 2)
            Ets = rot_sb(nc, es, "Et", [128, 8, 2, 128], BF16, 2)
            Gs = rot_sb(nc, es, "G6", [128, 8, 2, 128], BF16, 2)
            gels = rot_sb(nc, es, "gel", [128, 512], F32, 2)
            cts = rot_sb(nc, es, "ct6", [128, 512], BF16, 2)
            pAe = [rot_ps(nc, es, "p6a%d" % i, 2) for i in range(2)]
            pGe = [rot_ps(nc, es, "p6g%d" % i, 2) for i in range(2)]
            usrc = uT.rearrange("(fc p) e -> p fc e", p=128)
            for grp in groups:
                N = 128 * len(grp)
                c0 = grp[0] * 128
                for i in range(4):
                    S.dma("sp", xg6[:, 8 * i:8 * i + 8, 0:N], x1Tv[:, 8 * i:8 * i + 8, c0:c0 + N], writes=[b_xg6])
                for li, tt in enumerate(grp):
                    S.dma("sp", gate6[:, li, :], gate_s[tt], writes=[b_gate6])
                for ep in range(NEXP // 256):
                    e0 = ep * 256
                    ut, b_ut = uts.next()
                    for i in range(4):
                        S.dma("pool", ut[:, 8 * i:8 * i + 8, :], usrc[:, 8 * i:8 * i + 8, e0:e0 + 256], writes=[b_ut])
                    pa = [pAe[0].next(), pAe[1].next()]
                    pg = [pGe[0].next(), pGe[1].next()]
                    for eb in range(2):
                        for fc in range(32):
                            S.op("pe", lambda e: e.matmul(pa[eb][0][:, 0:N], ut[:, fc, eb * 128:(eb + 1) * 128], xg6[:, fc, 0:N], start=(fc == 0), stop=(fc == 31)),
                                 reads=[b_ut, b_xg6], writes=[pa[eb][1]])
                    for li, tt in enumerate(grp):
                        spp, b_spp = spps.next()
                        Et, b_Et = Ets.next()
                        G, b_G = Gs.next()
                        s1v = gate6[:, li, 0:1024].rearrange("p (h k) -> p h k", h=8)[:, :, 2 * ep:2 * ep + 2]
                        s2v = gate6[:, li, 1024:2048].rearrange("p (h k) -> p h k", h=8)
                        S.op("pool", lambda e: e.tensor_tensor(out=spp[:], in0=s2v.unsqueeze(2).broadcast_to([128, 8, 2, 128]),
                                                                in1=s1v.unsqueeze(3).broadcast_to([128, 8, 2, 128]), op=ALU.add),
                             reads=[b_gate6], writes=[b_spp])
                        for hh in range(8):
                            S.op("act", lambda e: e.activation(out=Et[:, hh, :, :], in_=spp[:, hh, :, :], func=AF.Exp,
                                                               bias=gate6[:, li, 2048 + hh:2049 + hh], scale=1.0),
                                 reads=[b_spp, b_gate6], writes=[b_Et])
                        S.op("dve", lambda e: e.scalar_tensor_tensor(out=G[:].rearrange("p h a k -> p (h a k)"), in0=spp[:].rearrange("p h a k -> p (h a k)"),
                                                                     scalar=0.0, in1=Et[:].rearrange("p h a k -> p (h a k)"), op0=ALU.is_ge, op1=ALU.mult),
                             reads=[b_spp, b_Et], writes=[b_G])
                        for eb in range(2):
                            for hh in range(8):
                                S.op("pe", lambda e: e.matmul(pg[eb][0][:, li * 128:(li + 1) * 128], G[:, hh, eb, :], ident[:], start=(hh == 0), stop=(hh == 7)),
                                     reads=[b_G, b_ident], writes=[pg[eb][1]])
                    for eb in range(2):
                        gel, b_gel = gels.next()
                        ct, b_ct = cts.next()
                        S.op("act", lambda e: e.activation(out=gel[:, 0:N], in_=pa[eb][0][:, 0:N], func=AF.Gelu), pr=[pa[eb][1]], writes=[b_gel])
                        S.op("dve", lambda e: e.tensor_tensor(out=ct[:, 0:N], in0=gel[:, 0:N], in1=pg[eb][0][:, 0:N], op=ALU.mult),
                             reads=[b_gel], pr=[pg[eb][1]], writes=[b_ct])
                        S.dma("sp", coefT_s[e0 + eb * 128:e0 + (eb + 1) * 128, c0:c0 + N], ct[:, 0:N], reads=[b_ct])
        S.barrier()

        with ExitStack() as es:
            c8s = rot_sb(nc, es, "c8", [128, 8, 512], BF16, 3)
            v8s = rot_sb(nc, es, "v8", [128, 8, 512], BF16, 3)
            x1s = rot_sb(nc, es, "x1sl", [128, 512], F32, 3)
            h2s = rot_sb(nc, es, "h2sl", [128, 512], F32, 3)
            accs = rot_ps(nc, es, "p6acc", 8)
            cTv = coefT_s.rearrange("(ec p) t -> p ec t", p=128)
            vv = v_tab.rearrange("(ec p) d -> p ec d", p=128)
            for grp in groups:
                N = 128 * len(grp)
                c0 = grp[0] * 128
                for ds_ in range(D // 512):
                    acc = [accs.next() for _ in grp]
                    for e8 in range(NEXP // 1024):
                        c8, b_c8 = c8s.next()
                        v8, b_v8 = v8s.next()
                        S.dma("sp", c8[:, :, 0:N], cTv[:, e8 * 8:(e8 + 1) * 8, c0:c0 + N], writes=[b_c8])
                        S.dma("pool", v8[:], vv[:, e8 * 8:(e8 + 1) * 8, ds_ * 512:(ds_ + 1) * 512], writes=[b_v8])
                        for k in range(8):
                            for li in range(len(grp)):
                                S.op("pe", lambda e: e.matmul(acc[li][0][:], c8[:, k, li * 128:(li + 1) * 128], v8[:, k, :],
                                                              start=(e8 == 0 and k == 0), stop=(e8 == NEXP // 1024 - 1 and k == 7)),
                                     reads=[b_c8, b_v8], writes=[acc[li][1]])
                    for li, tt in enumerate(grp):
                        x1t, b_x1t = x1s.next()
                        h2, b_h2 = h2s.next()
                        S.dma("sp", x1t[:], x1_s[tt * 128:(tt + 1) * 128, ds_ * 512:(ds_ + 1) * 512], writes=[b_x1t])
                        S.op("dve", lambda e: e.scalar_tensor_tensor(out=h2[:], in0=x1t[:], scalar=ALPHA, in1=acc[li][0][:], op0=ALU.mult, op1=ALU.add),
                             reads=[b_x1t], pr=[acc[li][1]], writes=[b_h2])
                        S.dma("sp", h_s[tt * 128:(tt + 1) * 128, ds_ * 512:(ds_ + 1) * 512], h2[:], reads=[b_h2])
        S.barrier()
        ln_pass(2, 3, False)
        S.barrier()
    return nc


def _rope_table(SEQ):
    inv = (np.float32(10000.0) ** (-np.arange(0, 128, 2, dtype=np.float32) / np.float32(128))).astype(np.float32)
    pos = np.concatenate([np.arange(SEQ), np.arange(SEQ), np.tile(PAST + np.arange(TS), NSB)]).astype(np.float32)
    ang = (pos[:, None] * inv[None, :]).astype(np.float32)
    return np.ascontiguousarray(np.concatenate([np.cos(ang), np.sin(ang)], axis=1).astype(np.float32))


def prep(inp, SEQ):
    NP = 2 * SEQ
    NTOK = NP + NS
    TPC = NP // NCORE
    SPC = NS // NCORE
    NGL = SEQ + NS // 2
    NT = TPC // 128 + 1
    RAG = NTOK + 2 * NGL
    f = lambda a: np.ascontiguousarray(np.asarray(a, dtype=np.float32))
    xall = np.concatenate([np.asarray(inp["x_prompt"]).reshape(NP, D), np.asarray(inp["x_sample"]).reshape(NS, D)], axis=0)
    xT = f(xall.T)
    cs = _rope_table(SEQ)
    w_in = np.asarray(inp["w_in"])[0]
    lam4 = f(np.concatenate([np.asarray(inp[k])[0] for k in ("lam_q1", "lam_k1", "lam_q2", "lam_k2")])[None, :])
    dng = f(np.asarray(inp["diff_norm_g"])[0][None, :])
    gng = f(np.asarray(inp["gla_norm_g"])[0][None, :])
    ck = np.asarray(inp["cache_diff_k"])[0]
    cv = np.asarray(inp["cache_diff_v"])[0]
    wg2_all = np.asarray(inp["w_gate2"])[0]
    bg_all = np.asarray(inp["b_gate"])[0]
    st_all = np.asarray(inp["state_gla"])[0]
    w_out = f(np.asarray(inp["w_out"])[0])
    lnp = f(np.stack([np.asarray(inp[k])[0] for k in ("ln1_g", "ln1_b", "ln2_g", "ln2_b")]))
    wq = f(np.asarray(inp["peer_wq"])[0])
    k1T = f(np.asarray(inp["peer_keys1"])[0].transpose(2, 0, 1))
    k2T = f(np.asarray(inp["peer_keys2"])[0].transpose(2, 0, 1))
    uT = f(np.asarray(inp["peer_u"])[0].T)
    v_tab = f(np.asarray(inp["peer_v"])[0])
    xTg = []
    for bb in range(2):
        xTg.append(f(np.concatenate([xT[:, bb * SEQ:(bb + 1) * SEQ], xT[:, NP + bb * 128:NP + bb * 128 + 128]], axis=1)))
    maps = []
    for c in range(NCORE):
        g, bb = c // 2, c % 2
        cols = np.concatenate([np.arange(c * 256, c * 256 + 256), 2048 + np.arange(c * 256, c * 256 + 256),
                               4096 + np.arange(c * 256, c * 256 + 256)])
        gcols = np.concatenate([6144 + np.arange(g * 256, g * 256 + 256), 7168 + np.arange(g * 256, g * 256 + 256),
                                8192 + np.arange(g * 512, g * 512 + 512), 10240 + np.arange(g * 512, g * 512 + 512),
                                12288 + np.arange(16)])
        x_own = np.zeros((NT * 128, D), np.float32)
        x_own[:TPC] = xall[c * TPC:(c + 1) * TPC]
        x_own[TPC:TPC + SPC] = xall[NP + c * SPC:NP + (c + 1) * SPC]
        gtok = np.zeros(NT * 128, np.int64)
        gtok[:TPC] = c * TPC + np.arange(TPC)
        gtok[TPC:TPC + SPC] = NP + c * SPC + np.arange(SPC)
        bbj = c // 4
        ltok = np.zeros(NT * 128, np.int64)
        ltok[:TPC] = gtok[:TPC] - bbj * SEQ
        ltok[TPC:TPC + SPC] = SEQ + (c * SPC + np.arange(SPC)) - bbj * 128
        gi = np.zeros((128, NT * 12), np.int32)
        for tt in range(NT):
            sl = slice(tt * 128, (tt + 1) * 128)
            for h in range(8):
                gi[:, tt * 12 + h] = h * RAG + gtok[sl]
            for gg in range(4):
                r = 2 * gg + bbj
                gi[:, tt * 12 + 8 + gg] = (r * RAG + NTOK) // 2 + ltok[sl]
        m = {
            "xT": xT, "cs_tab": cs, "w_diff": f(w_in[:, cols]), "lam4": lam4, "dng": dng,
            "cache_kT": f(ck[:, :, c, :].reshape(NSB, PAST, 2, 128).transpose(0, 2, 3, 1)),
            "cache_v": f(cv[:, :, c, :]),
            "xTg": xTg[bb], "w_gla": f(w_in[:, gcols]), "wg2": f(wg2_all[:, g * 256:(g + 1) * 256]),
            "bg": f(bg_all[g * 256:(g + 1) * 256][None, :]), "gng": gng,
            "state_s": f(st_all[bb * 8:(bb + 1) * 8, g]),
            "gidx": np.ascontiguousarray(gi), "x_own": x_own, "w_out": w_out, "lnp": lnp,
            "peer_wq": wq, "k1T": k1T, "k2T": k2T, "uT": uT, "v_tab": v_tab,
        }
        maps.append(m)
    return maps


_NC_CACHE = {}


def kernel(**inputs):
    SEQ = int(np.asarray(inputs["x_prompt"]).shape[1])
    NP = 2 * SEQ
    TPC = NP // NCORE
    SPC = NS // NCORE
    if SEQ not in _NC_CACHE:
        _NC_CACHE[SEQ] = build(SEQ)
    nc = _NC_CACHE[SEQ]
    maps = prep(inputs, SEQ)
    maps = [{k: m[k] for k in nc.in_names} for m in maps]
    res = run_bass_kernel_spmd(nc, maps, core_ids=list(range(NCORE)))
    R = res.results
    f = lambda a: np.asarray(a, dtype=np.float32)
    y_p = np.concatenate([f(R[c]["y_own"])[:TPC] for c in range(NCORE)], axis=0).reshape(2, SEQ, D)
    y_s = np.concatenate([f(R[c]["y_own"])[TPC:TPC + SPC] for c in range(NCORE)], axis=0).reshape(NSB, TS, D)
    k_all = np.stack([f(R[c]["k_out"]) for c in range(NCORE)], axis=1)
    v_all = np.stack([f(R[c]["v_out"]) for c in range(NCORE)], axis=1)
    k_p = np.ascontiguousarray(k_all[:NP]).reshape(1, 2, SEQ, 8, 256)
    v_p = np.ascontiguousarray(v_all[:NP]).reshape(1, 2, SEQ, 8, 256)
    k_s = np.ascontiguousarray(k_all[NP:]).reshape(1, NSB, TS, 8, 256)
    v_s = np.ascontiguousarray(v_all[NP:]).reshape(1, NSB, TS, 8, 256)
    g_p = np.zeros((1, 2, 4, 256, 512), np.float32)
    g_s = np.zeros((1, NSB, 4, 256, 512), np.float32)
    for c in range(NCORE):
        g, bb = c // 2, c % 2
        g_p[0, bb, g] = f(R[c]["gla_p"])
        g_s[0, bb * 8:(bb + 1) * 8, g] = f(R[c]["gla_s"])
    return (y_p, y_s, k_p, v_p, g_p, k_s, v_s, g_s)
```

```python
import math
from contextlib import ExitStack

import numpy as np
import concourse.bass as bass
import concourse.mybir as mybir
from concourse.bass_utils import run_bass_kernel_spmd

F32 = mybir.dt.float32
BF16 = mybir.dt.bfloat16
ALU = mybir.AluOpType
AF = mybir.ActivationFunctionType
AX = mybir.AxisListType

D = 4096
NCORE = 8
NSB = 16
TS = 16
PAST = 1024
NS = NSB * TS
EPS = 1e-5
ALPHA = 2.0 ** 0.25
LAM_INIT = 0.8 - 0.6 * math.exp(0.0)
NEXP = 16384
SCALE_D = 128 ** -0.5
SCALE_G = 256 ** -0.5


class Buf:
    __slots__ = ("name", "lw", "rd")

    def __init__(self, name=""):
        self.name = name
        self.lw = None
        self.rd = []


class Sched:
    def __init__(self, nc, es, ndma=8):
        self.nc = nc
        self.eng = {"pe": nc.tensor, "act": nc.scalar, "dve": nc.vector, "pool": nc.gpsimd, "sp": nc.sync}
        self.sem = {}
        self.cnt = {}
        for e in self.eng:
            self.sem[e] = es.enter_context(nc.semaphore("s_" + e))
            self.cnt[e] = 0
        self.waited = {e: {} for e in self.eng}
        self.dq = {}
        for q in ("sp", "pool"):
            sems = [es.enter_context(nc.semaphore("d_%s%d" % (q, i))) for i in range(ndma)]
            self.dq[q] = {"sems": sems, "val": [0] * ndma, "tok": [None] * ndma, "i": 0}
        self.sems_by_key = {}
        for e in self.eng:
            self.sems_by_key[e] = self.sem[e]
        for q in self.dq:
            for i, s in enumerate(self.dq[q]["sems"]):
                self.sems_by_key[(q, i)] = s
        self.nins = 0

    def _wait(self, e, tok):
        if tok is None:
            return
        key, val = tok
        if self.waited[e].get(key, 0) >= val:
            return
        self.waited[e][key] = val
        self.eng[e].wait_ge(self.sems_by_key[key], val)

    def _deps(self, reads, writes):
        toks = []
        for b in reads:
            if b.lw is not None:
                toks.append(b.lw)
        for b in writes:
            if b.lw is not None:
                toks.append(b.lw)
            toks.extend(b.rd)
        return toks

    def _commit(self, tok, reads, writes):
        for b in reads:
            b.rd.append(tok)
        for b in writes:
            b.lw = tok
            b.rd = []

    def op(self, e, fn, reads=(), writes=(), pr=()):
        toks = self._deps(reads, writes)
        for b in pr:
            if b.lw is not None:
                toks.append(b.lw)
            toks.extend(t for t in b.rd if t[0] != e)
        for tok in toks:
            if e == "pe" and tok[0] == "pe":
                continue
            self._wait(e, tok)
        ins = fn(self.eng[e])
        self.cnt[e] += 1
        ins.then_inc(self.sem[e], 1)
        tok = (e, self.cnt[e])
        self._commit(tok, list(reads) + list(pr), writes)
        self.nins += 1
        return tok

    def dma(self, q, out, in_, reads=(), writes=(), **kw):
        d = self.dq[q]
        i = d["i"]
        d["i"] = (i + 1) % len(d["sems"])
        self._wait(q, d["tok"][i])
        for tok in self._deps(reads, writes):
            self._wait(q, tok)
        ins = self.eng[q].dma_start(out=out, in_=in_, **kw)
        d["val"][i] += 16
        ins.then_inc(d["sems"][i], 16)
        tok = ((q, i), d["val"][i])
        d["tok"][i] = tok
        self._commit(tok, reads, writes)
        self.nins += 1
        return tok

    def gather(self, out, in_, idx, reads=(), writes=()):
        q = "pool"
        d = self.dq[q]
        i = d["i"]
        d["i"] = (i + 1) % len(d["sems"])
        self._wait(q, d["tok"][i])
        for tok in self._deps(reads, writes):
            self._wait(q, tok)
        ins = self.nc.gpsimd.indirect_dma_start(out=out, out_offset=None, in_=in_,
                                                in_offset=bass.IndirectOffsetOnAxis(ap=idx, axis=0))
        d["val"][i] += 16
        ins.then_inc(d["sems"][i], 16)
        tok = ((q, i), d["val"][i])
        d["tok"][i] = tok
        self._commit(tok, reads, writes)
        self.nins += 1
        return tok

    def all_tokens(self):
        toks = [(e, self.cnt[e]) for e in self.eng if self.cnt[e] > 0]
        for q, d in self.dq.items():
            toks.extend(t for t in d["tok"] if t is not None)
        return toks

    def barrier(self, engines=None):
        toks = self.all_tokens()
        for e in (engines or self.eng):
            for t in toks:
                if t[0] == e:
                    continue
                self._wait(e, t)


class Rot:
    def __init__(self, tiles):
        self.tiles = tiles
        self.bufs = [Buf() for _ in tiles]
        self.i = 0

    def next(self):
        t, b = self.tiles[self.i], self.bufs[self.i]
        self.i = (self.i + 1) % len(self.tiles)
        return t, b


_UID = [0]


def _uname(name):
    _UID[0] += 1
    return "%s_%d" % (name, _UID[0])


def sb(nc, es, name, shape, dt):
    return es.enter_context(nc.sbuf_tensor(_uname(name), list(shape), dt))


def ps(nc, es, name, shape, dt=F32):
    return es.enter_context(nc.psum_tensor(_uname(name), list(shape), dt))


def rot_sb(nc, es, name, shape, dt, n):
    return Rot([sb(nc, es, "%s%d" % (name, i), shape, dt) for i in range(n)])


def rot_ps(nc, es, name, n):
    return Rot([ps(nc, es, "%s%d" % (name, i), [128, 512], F32) for i in range(n)])


def bcast_rows(ap1, n):
    return ap1.broadcast_to([n, ap1.shape[-1]])


def build(SEQ, debug=False, stop=99):
    NP = 2 * SEQ
    NTOK = NP + NS
    TPC = NP // NCORE
    SPC = NS // NCORE
    NOWN = TPC + SPC
    NGL = SEQ + NS // 2
    assert SEQ % 256 == 0 and TPC % 128 == 0

    nc = bass.Bass("TRN2", target_bir_lowering=False)

    in_names = []

    def din(name, shape, dt=F32, need=0):
        if stop < need:
            return None
        in_names.append(name)
        return nc.dram_tensor(name, list(shape), dt, kind="ExternalInput").ap()

    nc.in_names = in_names

    def dout(name, shape, dt=F32):
        return nc.dram_tensor(name, list(shape), dt, kind="ExternalOutput").ap()

    xT = din("xT", [D, NTOK])
    cs_tab = din("cs_tab", [NTOK, 128])
    w_diff = din("w_diff", [D, 768])
    lam4 = din("lam4", [1, 512])
    dng = din("dng", [1, 256])
    cache_kT = din("cache_kT", [NSB, 2, 128, PAST])
    cache_v = din("cache_v", [NSB, PAST, 256])

    xTg = din("xTg", [D, NGL])
    w_gla = din("w_gla", [D, 1552])
    wg2 = din("wg2", [16, 256])
    bg = din("bg", [1, 256])
    gng = din("gng", [1, 512])
    state_s = din("state_s", [8, 256, 512])
    NT = TPC // 128 + 1
    NPAD = NT * 128
    gidx = din("gidx", [128, NT * 12], mybir.dt.int32)
    x_own = din("x_own", [NPAD, D])
    w_out = din("w_out", [D, D])
    lnp = din("lnp", [4, D])
    peer_wq = din("peer_wq", [D, 2048], need=4.5)
    k1T = din("k1T", [128, 8, 128])
    k2T = din("k2T", [128, 8, 128])
    uT = din("uT", [D, NEXP], need=5.5)
    v_tab = din("v_tab", [NEXP, D], need=5.5)

    k_out = dout("k_out", [NTOK, 256])
    v_out = dout("v_out", [NTOK, 256])
    gla_p = dout("gla_p", [256, 512])
    gla_s = dout("gla_s", [8, 256, 512])
    y_own = dout("y_own", [NPAD, D])
    RAG = NTOK + 2 * NGL
    ag_in = nc.dram_tensor("ag_in", [RAG, 256], BF16).ap()
    ag_out = nc.dram_tensor("ag_out", [NCORE * RAG, 256], BF16).ap()
    h_s = nc.dram_tensor("h_s", [NPAD, D], F32).ap()
    x1_s = nc.dram_tensor("x1_s", [NPAD, D], F32).ap()
    x1T_s = nc.dram_tensor("x1T_s", [D, NPAD], BF16).ap()
    gate_s = nc.dram_tensor("gate_s", [NT, 128, 2056], F32).ap()
    coefT_s = nc.dram_tensor("coefT_s", [NEXP, NPAD], BF16).ap()
    if debug:
        dbg_ag = dout("dbg_ag", [RAG, 256], BF16)
        dbg_x1 = dout("dbg_x1", [NPAD, D])
        dbg_gate = dout("dbg_gate", [NT, 128, 2056])

    qT_s = nc.dram_tensor("qT_s", [2, 128, NTOK], BF16).ap()
    kT_s = nc.dram_tensor("kT_s", [2, 128, NTOK], BF16).ap()
    v_s = nc.dram_tensor("v_s", [NTOK, 256], BF16).ap()

    with ExitStack() as es0:
        S = Sched(nc, es0)
        ident = sb(nc, es0, "ident", [128, 128], BF16)
        b_ident = Buf()
        S.op("pool", lambda e: e.memset(ident[:], 0.0), writes=[b_ident])
        S.op("pool", lambda e: e.affine_select(out=ident[:], in_=ident[:], pattern=[[-1, 128]],
                                                compare_op=ALU.not_equal, fill=1.0, base=0,
                                                channel_multiplier=1), reads=[b_ident], writes=[b_ident])
        b_const = Buf()
        lamt = sb(nc, es0, "lamt", [128, 512], F32)
        lamw = sb(nc, es0, "lamw", [128, 256], F32)
        lams = sb(nc, es0, "lams", [128, 4], F32)
        gd = sb(nc, es0, "gd", [128, 256], F32)
        S.dma("sp", lamt[:], bcast_rows(lam4, 128), writes=[b_const])
        S.dma("sp", gd[:], bcast_rows(dng, 128), writes=[b_const])
        S.op("dve", lambda e: e.tensor_tensor(out=lamw[:, 0:128], in0=lamt[:, 0:128], in1=lamt[:, 128:256], op=ALU.mult),
             reads=[b_const], writes=[b_const])
        S.op("dve", lambda e: e.tensor_tensor(out=lamw[:, 128:256], in0=lamt[:, 256:384], in1=lamt[:, 384:512], op=ALU.mult),
             reads=[b_const], writes=[b_const])
        S.op("dve", lambda e: e.tensor_reduce(out=lams[:, 0:2], in_=lamw[:].rearrange("p (a d) -> p a d", a=2),
                                               axis=AX.X, op=ALU.add), reads=[b_const], writes=[b_const])
        S.op("act", lambda e: e.activation(out=lams[:, 2:4], in_=lams[:, 0:2], func=AF.Exp), reads=[b_const], writes=[b_const])
        S.op("dve", lambda e: e.tensor_tensor(out=lams[:, 0:1], in0=lams[:, 3:4], in1=lams[:, 2:3], op=ALU.subtract),
             reads=[b_const], writes=[b_const])
        S.op("dve", lambda e: e.tensor_scalar(out=lams[:, 3:4], in0=lams[:, 0:1], scalar1=-LAM_INIT, scalar2=None, op0=ALU.add),
             reads=[b_const], writes=[b_const])
        S.op("dve", lambda e: e.tensor_scalar(out=gd[:], in0=gd[:], scalar1=1.0 - LAM_INIT, scalar2=None, op0=ALU.mult),
             reads=[b_const], writes=[b_const])
        neglam = lams[:, 3:4]
        if stop <= 0:
            S.barrier()
            return nc

        with ExitStack() as es:
            wd = sb(nc, es, "wd", [128, 32, 768], BF16)
            b_wd = Buf()
            wsrc = w_diff.rearrange("(kc p) n -> p kc n", p=128)
            for i in range(4):
                S.dma("pool", wd[:, 8 * i:8 * i + 8, :], wsrc[:, 8 * i:8 * i + 8, :], writes=[b_wd])
            xts = rot_sb(nc, es, "xt", [128, 32, 512], BF16, 2)
            css = rot_sb(nc, es, "cs", [128, 4, 128], F32, 2)
            tbs = rot_sb(nc, es, "tb", [128, 4, 512], BF16, 2)
            p_qk = rot_ps(nc, es, "p_qk", 2)
            p_v = rot_ps(nc, es, "p_v", 2)
            p_tr = rot_ps(nc, es, "p_tr", 2)
            rks = rot_sb(nc, es, "rk", [128, 512], F32, 2)
            tmps = rot_sb(nc, es, "rtmp", [128, 512], F32, 2)
            rkbs = rot_sb(nc, es, "rkb", [128, 512], BF16, 2)
            vfs = rot_sb(nc, es, "vf", [128, 256], F32, 2)
            vbs = rot_sb(nc, es, "vb", [128, 256], BF16, 2)
            xsrc = xT.rearrange("(kc p) t -> p kc t", p=128)
            t0 = 0
            while t0 < NTOK:
                n = min(512, NTOK - t0)
                xt, b_xt = xts.next()
                for i in range(4):
                    S.dma("pool", xt[:, 8 * i:8 * i + 8, 0:n], xsrc[:, 8 * i:8 * i + 8, t0:t0 + n], writes=[b_xt])
                cs, b_cs = css.next()
                nj = n // 128
                S.dma("sp", cs[:, 0:nj, :], cs_tab[t0:t0 + n, :].rearrange("(j p) d -> p j d", p=128), writes=[b_cs])
                tb, b_tb = tbs.next()
                for j in range(nj if stop >= 0.6 else 0):
                    tt0 = t0 + j * 128
                    pqk, b_pqk = p_qk.next()
                    pv, b_pv = p_v.next()
                    for kc in range(32):
                        S.op("pe", lambda e, kc=kc: e.matmul(pqk[:], xt[:, kc, j * 128:(j + 1) * 128], wd[:, kc, 0:512],
                                                             start=(kc == 0), stop=(kc == 31)),
                             reads=[b_xt, b_wd], writes=[b_pqk])
                    for kc in range(32):
                        S.op("pe", lambda e, kc=kc: e.matmul(pv[:, 0:256], xt[:, kc, j * 128:(j + 1) * 128], wd[:, kc, 512:768],
                                                             start=(kc == 0), stop=(kc == 31)),
                             reads=[b_xt, b_wd], writes=[b_pv])
                    if stop < 0.7:
                        continue
                    rk, b_rk = rks.next()
                    tmp, b_tmp = tmps.next()
                    q4 = pqk[:].rearrange("p (a h d) -> p a h d", a=4, h=2)
                    r4 = rk[:].rearrange("p (a h d) -> p a h d", a=4, h=2)
                    t4 = tmp[:].rearrange("p (a h d) -> p a h d", a=4, h=2)
                    cosb = cs[:, j, 0:64].unsqueeze(1).broadcast_to([128, 4, 64])
                    sinb = cs[:, j, 64:128].unsqueeze(1).broadcast_to([128, 4, 64])
                    for h in range(2):
                        S.op("dve", lambda e, h=h: e.tensor_tensor(out=r4[:, :, h, :], in0=q4[:, :, h, :], in1=cosb, op=ALU.mult),
                             reads=[b_cs], pr=[b_pqk], writes=[b_rk])
                        S.op("dve", lambda e, h=h: e.tensor_tensor(out=t4[:, :, h, :], in0=q4[:, :, 1 - h, :], in1=sinb, op=ALU.mult),
                             reads=[b_cs], pr=[b_pqk], writes=[b_tmp])
                    S.op("dve", lambda e: e.tensor_tensor(out=r4[:, :, 0, :], in0=r4[:, :, 0, :], in1=t4[:, :, 0, :], op=ALU.subtract),
                         reads=[b_rk, b_tmp], writes=[b_rk])
                    S.op("dve", lambda e: e.tensor_tensor(out=r4[:, :, 1, :], in0=r4[:, :, 1, :], in1=t4[:, :, 1, :], op=ALU.add),
                         reads=[b_rk, b_tmp], writes=[b_rk])
                    if stop < 0.8:
                        continue
                    S.dma("sp", k_out[tt0:tt0 + 128, :], rk[:, 256:512], reads=[b_rk])
                    rkb, b_rkb = rkbs.next()
                    S.op("act", lambda e: e.copy(out=rkb[:], in_=rk[:]), reads=[b_rk], writes=[b_rkb])
                    vf, b_vf = vfs.next()
                    vb, b_vb = vbs.next()
                    S.op("act", lambda e: e.copy(out=vf[:], in_=pv[:, 0:256]), pr=[b_pv], writes=[b_vf])
                    S.op("act", lambda e: e.copy(out=vb[:], in_=vf[:]), reads=[b_vf], writes=[b_vb])
                    S.dma("sp", v_out[tt0:tt0 + 128, :], vf[:], reads=[b_vf])
                    S.dma("sp", v_s[tt0:tt0 + 128, :], vb[:], reads=[b_vb])
                    if stop < 0.9:
                        continue
                    ptr, b_ptr = p_tr.next()
                    for a in range(4):
                        S.op("pe", lambda e, a=a: e.matmul(ptr[:, a * 128:(a + 1) * 128], rkb[:, a * 128:(a + 1) * 128], ident[:], start=True, stop=True),
                             reads=[b_rkb, b_ident], writes=[b_ptr])
                    S.op("act", lambda e: e.copy(out=tb[:, :, j * 128:(j + 1) * 128], in_=ptr[:].rearrange("p (a t) -> p a t", a=4)),
                         pr=[b_ptr], writes=[b_tb])
                for a in range(4 if stop >= 0.9 else 0):
                    dst = (qT_s if a < 2 else kT_s)[a % 2, :, t0:t0 + n]
                    S.dma("sp", dst, tb[:, a, 0:n], reads=[b_tb])
                t0 += n
        S.barrier()
        if stop <= 1:
            return nc

        with ExitStack() as es:
            MAXK = max(SEQ, PAST + TS)
            KT = sb(nc, es, "KT", [128, 2, MAXK], BF16)
            QT = sb(nc, es, "QT", [128, 2, SEQ], BF16)
            NB = MAXK // 128 + 1
            V = sb(nc, es, "V", [128, NB, 257], BF16)
            b_KT, b_QT, b_V = Buf(), Buf(), Buf()
            S.op("pool", lambda e: e.memset(V[:], 1.0), writes=[b_V])
            p_s = rot_ps(nc, es, "p_s", 2)
            p_o = [[ps(nc, es, "p_o%d%d" % (a, i), [128, 512]) for i in range(2)] for a in range(2)]
            b_po = [[Buf(), Buf()], [Buf(), Buf()]]
            Es = rot_sb(nc, es, "E", [128, 2, 256], BF16, 3)
            p_t2 = ps(nc, es, "p_t2", [128, 512])
            b_pt2 = Buf()
            fo = rot_sb(nc, es, "fo", [128, 256], F32, 2)
            fsq = rot_sb(nc, es, "fsq", [128, 256], F32, 2)
            fr = rot_sb(nc, es, "fr", [128, 8], F32, 2)
            fob = rot_sb(nc, es, "fob", [128, 256], BF16, 2)
            foT = rot_sb(nc, es, "foT", [128, 2, 128], BF16, 2)

            def finalize(nq, sq, tok0):
                o, b_o = fo.next()
                r, b_r = fr.next()
                sqt, b_sq = fsq.next()
                ob, b_ob = fob.next()
                oT, b_oT = foT.next()
                po0, po1 = p_o[sq][0], p_o[sq][1]
                bo0, bo1 = b_po[sq][0], b_po[sq][1]
                S.op("dve", lambda e: e.reciprocal(out=r[0:nq, 0:1], in_=po0[0:nq, 256:257]), pr=[bo0], writes=[b_r])
                S.op("dve", lambda e: e.reciprocal(out=r[0:nq, 1:2], in_=po1[0:nq, 256:257]), pr=[bo1], writes=[b_r])
                S.op("dve", lambda e: e.tensor_tensor(out=r[0:nq, 2:3], in0=r[0:nq, 1:2], in1=neglam[0:nq, :], op=ALU.mult),
                     reads=[b_r, b_const], writes=[b_r])
                S.op("dve", lambda e: e.tensor_scalar(out=o[0:nq, :], in0=po0[0:nq, 0:256], scalar1=r[0:nq, 0:1], scalar2=None, op0=ALU.mult),
                     reads=[b_r], pr=[bo0], writes=[b_o])
                S.op("dve", lambda e: e.scalar_tensor_tensor(out=o[0:nq, :], in0=po1[0:nq, 0:256], scalar=r[0:nq, 2:3], in1=o[0:nq, :],
                                                             op0=ALU.mult, op1=ALU.add), reads=[b_r, b_o], pr=[bo1], writes=[b_o])
                S.op("dve", lambda e: e.tensor_tensor(out=sqt[0:nq, :], in0=o[0:nq, :], in1=o[0:nq, :], op=ALU.mult),
                     reads=[b_o], writes=[b_sq])
                S.op("dve", lambda e: e.tensor_reduce(out=r[0:nq, 3:4], in_=sqt[0:nq, :], axis=AX.X, op=ALU.add),
                     reads=[b_sq, b_r], writes=[b_r])
                S.op("act", lambda e: e.activation(out=r[0:nq, 4:5], in_=r[0:nq, 3:4], func=AF.Ln, scale=1.0 / 256, bias=eps_t[0:nq, :]),
                     reads=[b_r, b_const], writes=[b_r])
                S.op("act", lambda e: e.activation(out=r[0:nq, 5:6], in_=r[0:nq, 4:5], func=AF.Exp, scale=-0.5),
                     reads=[b_r], writes=[b_r])
                S.op("dve", lambda e: e.scalar_tensor_tensor(out=ob[0:nq, :], in0=o[0:nq, :], scalar=r[0:nq, 5:6], in1=gd[0:nq, :],
                                                             op0=ALU.mult, op1=ALU.mult), reads=[b_o, b_r, b_const], writes=[b_ob])
                S.dma("sp", ag_in[tok0:tok0 + nq, :], ob[0:nq, :], reads=[b_ob])

            eps_t = sb(nc, es, "eps_t", [128, 1], F32)
            S.op("pool", lambda e: e.memset(eps_t[:], EPS), writes=[b_const])

            for b in range(2):
                base = b * SEQ
                for i in range(2):
                    S.dma("sp", KT[:, i, 0:SEQ], kT_s[i, :, base:base + SEQ], writes=[b_KT])
                    S.dma("sp", QT[:, i, 0:SEQ], qT_s[i, :, base:base + SEQ], writes=[b_QT])
                nblk = SEQ // 128
                for c0 in range(0, nblk, 8):
                    c1 = min(nblk, c0 + 8)
                    S.dma("sp", V[:, c0:c1, 0:256],
                          v_s[base + c0 * 128:base + c1 * 128, :].rearrange("(k p) d -> p k d", p=128), writes=[b_V])
                for qg in range(SEQ // 256):
                    q0 = qg * 256
                    last = 2 * qg + 1

                    def score(kb):
                        pst_, b_ps = p_s.next()
                        pst = pst_[:].rearrange("p (i q) -> p i q", i=2)
                        E, b_E = Es.next()
                        lo = 128 if kb == last else 0
                        for i in range(2):
                            S.op("pe", lambda e, i=i: e.matmul(pst[:, i, lo:256], KT[:, i, kb * 128:(kb + 1) * 128],
                                                               QT[:, i, q0 + lo:q0 + 256], start=True, stop=True),
                                 reads=[b_KT, b_QT], writes=[b_ps])
                        S.op("act", lambda e: e.activation(out=E[:, :, lo:256], in_=pst[:, :, lo:256], func=AF.Exp, scale=SCALE_D),
                             pr=[b_ps], writes=[b_E])
                        for sq in range(2):
                            if kb == 2 * qg + sq:
                                S.op("pool", lambda e, sq=sq: e.memset(E[64:128, :, sq * 128:sq * 128 + 64], 0.0),
                                     reads=[b_E], writes=[b_E])
                        return E, b_E

                    def pv(kb, E, b_E):
                        for sq in range(2):
                            if kb > 2 * qg + sq:
                                continue
                            for i in range(2):
                                S.op("pe", lambda e, i=i, sq=sq: e.matmul(p_o[sq][i][:, 0:257], E[:, i, sq * 128:(sq + 1) * 128],
                                                                        V[:, kb, :], start=(kb == 0), stop=(kb == 2 * qg + sq)),
                                     reads=[b_E, b_V], writes=[b_po[sq][i]])

                    cur = score(0)
                    for kb in range(last + 1):
                        nxt = score(kb + 1) if kb + 1 <= last else None
                        pv(kb, *cur)
                        cur = nxt
                    for sq in range(2):
                        finalize(128, sq, base + q0 + sq * 128)

            for s_ in range(NSB):
                tok0 = NP + s_ * TS
                for i in range(2):
                    S.dma("pool", KT[:, i, 0:PAST], cache_kT[s_, i, :, :], writes=[b_KT])
                    S.dma("sp", KT[:, i, PAST:PAST + TS], kT_s[i, :, tok0:tok0 + TS], writes=[b_KT])
                    S.dma("sp", QT[:, i, 0:TS], qT_s[i, :, tok0:tok0 + TS], writes=[b_QT])
                S.dma("pool", V[:, 0:8, 0:256], cache_v[s_].rearrange("(k p) d -> p k d", p=128), writes=[b_V])
                S.dma("sp", V[0:TS, 8, 0:256], v_s[tok0:tok0 + TS, :], writes=[b_V])
                for kb in range(9):
                    nk = 128 if kb < 8 else TS
                    pst_, b_ps = p_s.next()
                    pst = pst_[:].rearrange("p (i q) -> p i q", i=2)
                    E, b_E = Es.next()
                    for i in range(2):
                        S.op("pe", lambda e, i=i: e.matmul(pst[0:nk, i, 0:TS], KT[:, i, kb * 128:kb * 128 + nk],
                                                           QT[:, i, 0:TS], start=True, stop=True),
                             reads=[b_KT, b_QT], writes=[b_ps])
                    S.op("act", lambda e: e.activation(out=E[0:nk, :, 0:TS], in_=pst[0:nk, :, 0:TS], func=AF.Exp, scale=SCALE_D),
                         pr=[b_ps], writes=[b_E])
                    for i in range(2):
                        S.op("pe", lambda e, i=i: e.matmul(p_o[0][i][0:TS, 0:257], E[0:nk, i, 0:TS], V[0:nk, kb, :],
                                                           start=(kb == 0), stop=(kb == 8)),
                             reads=[b_E, b_V], writes=[b_po[0][i]])
                finalize(TS, 0, tok0)
        S.barrier()
        if stop <= 2:
            if debug:
                S.dma("sp", dbg_ag, ag_in)
                S.barrier()
            return nc
        with ExitStack() as es:
            wg = sb(nc, es, "wg", [128, 32, 1552], BF16)
            b_wg = Buf()
            wgsrc = w_gla.rearrange("(kc p) n -> p kc n", p=128)
            for i in range(8):
                S.dma("pool", wg[:, 4 * i:4 * i + 4, :], wgsrc[:, 4 * i:4 * i + 4, :], writes=[b_wg])
            b_gc = Buf()
            wg2b = sb(nc, es, "wg2b", [16, 256], BF16)
            bgt = sb(nc, es, "bgt", [128, 256], F32)
            ggt = sb(nc, es, "ggt", [128, 512], F32)
            S.dma("pool", wg2b[:], wg2, writes=[b_gc])
            S.dma("sp", bgt[:], bcast_rows(bg, 128), writes=[b_gc])
            S.dma("sp", ggt[:], bcast_rows(gng, 128), writes=[b_gc])
            eps_g = sb(nc, es, "eps_g", [128, 1], F32)
            S.op("pool", lambda e: e.memset(eps_g[:], EPS), writes=[b_gc])

            def make_masks(L, tag):
                nch = 128 // L
                tri = sb(nc, es, "tri" + tag, [128, 128], F32)
                dm = sb(nc, es, "dm" + tag, [128, 128], F32)
                rm = sb(nc, es, "rm" + tag, [128, nch], F32)
                S.op("pool", lambda e: e.memset(tri[:], 1.0), writes=[b_gc])
                S.op("pool", lambda e: e.memset(dm[:], 1.0), writes=[b_gc])
                S.op("pool", lambda e: e.memset(rm[:], 1.0), writes=[b_gc])
                t3 = tri[:].rearrange("p (c l) -> p c l", c=nch)
                d3 = dm[:].rearrange("p (c l) -> p c l", c=nch)
                S.op("pool", lambda e: e.affine_select(out=tri[:], in_=tri[:], pattern=[[1, 128]], compare_op=ALU.is_ge,
                                                        fill=0.0, base=0, channel_multiplier=-1), reads=[b_gc], writes=[b_gc])
                S.op("pool", lambda e: e.affine_select(out=t3, in_=t3, pattern=[[-L, nch], [0, L]], compare_op=ALU.is_ge,
                                                        fill=0.0, base=0, channel_multiplier=1), reads=[b_gc], writes=[b_gc])
                S.op("pool", lambda e: e.affine_select(out=dm[:], in_=dm[:], pattern=[[-1, 128]], compare_op=ALU.is_ge,
                                                        fill=0.0, base=-1, channel_multiplier=1), reads=[b_gc], writes=[b_gc])
                S.op("pool", lambda e: e.affine_select(out=d3, in_=d3, pattern=[[L, nch], [0, L]], compare_op=ALU.is_ge,
                                                        fill=0.0, base=L - 1, channel_multiplier=-1), reads=[b_gc], writes=[b_gc])
                S.op("pool", lambda e: e.affine_select(out=rm[:], in_=rm[:], pattern=[[-L, nch]], compare_op=ALU.is_ge,
                                                        fill=0.0, base=0, channel_multiplier=1), reads=[b_gc], writes=[b_gc])
                S.op("pool", lambda e: e.affine_select(out=rm[:], in_=rm[:], pattern=[[L, nch]], compare_op=ALU.is_ge,
                                                        fill=0.0, base=L - 1, channel_multiplier=-1), reads=[b_gc], writes=[b_gc])
                qtm = []
                for j in range(nch):
                    t = sb(nc, es, "qtm%s_%d" % (tag, j), [128, 2, 128], BF16)
                    S.op("pool", lambda e, t=t: e.memset(t[:], 0.0), writes=[b_gc])
                    qtm.append((t, Buf()))
                return tri, dm, rm, qtm

            masks = {64: make_masks(64, "a"), 16: make_masks(16, "b")}

            xgs = rot_sb(nc, es, "xg", [128, 32, 256], BF16, 2)
            pA = ps(nc, es, "pA", [128, 512]); pB = ps(nc, es, "pB", [128, 512]); pC = ps(nc, es, "pC", [128, 512])
            pD = ps(nc, es, "pD", [128, 512]); pE = ps(nc, es, "pE", [128, 512]); pF = ps(nc, es, "pF", [128, 512])
            pG = ps(nc, es, "pG", [128, 512]); pO = ps(nc, es, "pO", [128, 512])
            b_pA, b_pB, b_pC, b_pD, b_pE, b_pF, b_pG, b_pO = [Buf() for _ in range(8)]
            gzb = sb(nc, es, "gzb", [128, 16], BF16); b_gzb = Buf()
            gzT = sb(nc, es, "gzT", [16, 128], BF16); b_gzT = Buf()
            vbs = rot_sb(nc, es, "gvb", [128, 512], BF16, 2)
            sgs = rot_sb(nc, es, "gsg", [128, 512], F32, 2)
            zt = sb(nc, es, "zt", [128, 256], F32); b_zt = Buf()
            lt_ = sb(nc, es, "lt_", [128, 256], F32); b_lt = Buf()
            ebt = sb(nc, es, "ebt", [128, 256], F32); b_eb = Buf()
            enbt = sb(nc, es, "enbt", [128, 256], F32); b_enb = Buf()
            edt = sb(nc, es, "edt", [128, 256], F32); b_ed = Buf()
            eblt = sb(nc, es, "eblt", [128, 2, 8], F32); b_ebl = Buf()
            qin = sb(nc, es, "qin", [128, 256], BF16); b_qin = Buf()
            kin = sb(nc, es, "kin", [128, 256], BF16); b_kin = Buf()
            kend = sb(nc, es, "kend", [128, 256], F32); b_kend = Buf()
            kms = rot_sb(nc, es, "km", [128, 256], BF16, 2)
            qTf = sb(nc, es, "qTf", [128, 2, 128], BF16); b_qTf = Buf()
            kTf = sb(nc, es, "kTf", [128, 2, 128], BF16); b_kTf = Buf()
            aTm = sb(nc, es, "aTm", [128, 128], BF16); b_aTm = Buf()
            Sfs = rot_sb(nc, es, "Sf", [128, 2, 512], F32, 2)
            Sbs = rot_sb(nc, es, "Sb", [128, 2, 512], BF16, 2)
            of_ = sb(nc, es, "of_", [128, 512], F32); b_of = Buf()
            osq = sb(nc, es, "osq", [128, 512], F32); b_osq = Buf()
            orr = sb(nc, es, "orr", [128, 4], F32); b_orr = Buf()
            obs = rot_sb(nc, es, "gob", [128, 512], BF16, 2)
            xgsrc = xTg.rearrange("(kc p) t -> p kc t", p=128)
            ag_g = ag_in[NTOK:RAG, :].rearrange("(t two) d -> t (two d)", two=2)

            def gla_tile(xt, b_xt, col0, L, ltok0, state):
                nch = 128 // L
                tri, dm, rm, qtm = masks[L]
                xs = lambda kc: xt[:, kc, col0:col0 + 128]
                for kc in range(32):
                    st, sp_ = (kc == 0), (kc == 31)
                    S.op("pe", lambda e, kc=kc: e.matmul(pA[:], xs(kc), wg[:, kc, 0:512], start=st, stop=sp_), reads=[b_xt, b_wg], writes=[b_pA])
                    S.op("pe", lambda e, kc=kc: e.matmul(pB[:], xs(kc), wg[:, kc, 512:1024], start=st, stop=sp_), reads=[b_xt, b_wg], writes=[b_pB])
                    S.op("pe", lambda e, kc=kc: e.matmul(pC[:], xs(kc), wg[:, kc, 1024:1536], start=st, stop=sp_), reads=[b_xt, b_wg], writes=[b_pC])
                    S.op("pe", lambda e, kc=kc: e.matmul(pD[:, 0:16], xs(kc), wg[:, kc, 1536:1552], start=st, stop=sp_), reads=[b_xt, b_wg], writes=[b_pD])
                vb, b_vb = vbs.next()
                sg, b_sg = sgs.next()
                S.op("act", lambda e: e.copy(out=gzb[:], in_=pD[:, 0:16]), pr=[b_pD], writes=[b_gzb])
                S.op("act", lambda e: e.copy(out=vb[:], in_=pB[:]), pr=[b_pB], writes=[b_vb])
                S.op("pe", lambda e: e.matmul(pD[0:16, 128:256], gzb[:], ident[:], start=True, stop=True), reads=[b_gzb, b_ident], writes=[b_pD])
                S.op("dve", lambda e: e.tensor_copy(out=gzT[:], in_=pD[0:16, 128:256]), pr=[b_pD], writes=[b_gzT])
                S.op("pe", lambda e: e.matmul(pE[:, 0:256], gzT[:], wg2b[:], start=True, stop=True), reads=[b_gzT, b_gc], writes=[b_pE])
                S.op("dve", lambda e: e.tensor_tensor(out=zt[:], in0=pE[:, 0:256], in1=bgt[:], op=ALU.add), reads=[b_gc], pr=[b_pE], writes=[b_zt])
                S.op("act", lambda e: e.activation(out=lt_[:], in_=zt[:], func=AF.Exp, scale=-1.0), reads=[b_zt], writes=[b_lt])
                S.op("act", lambda e: e.activation(out=lt_[:], in_=lt_[:], func=AF.Ln, bias=1.0, scale=1.0), reads=[b_lt], writes=[b_lt])
                S.op("pe", lambda e: e.matmul(pE[:, 256:512], tri[:], lt_[:], start=True, stop=True), reads=[b_lt, b_gc], writes=[b_pE])
                S.op("pe", lambda e: e.matmul(pF[:, 0:256], dm[:], lt_[:], start=True, stop=True), reads=[b_lt, b_gc], writes=[b_pF])
                for hh in range(2):
                    S.op("pe", lambda e, hh=hh: e.matmul(pF[:, 256 + 8 * hh:256 + 8 * hh + nch], lt_[:, hh * 128:(hh + 1) * 128], rm[:],
                                                       start=True, stop=True), reads=[b_lt, b_gc], writes=[b_pF])
                S.op("act", lambda e: e.activation(out=ebt[:], in_=pE[:, 256:512], func=AF.Exp, scale=-1.0 / 16), pr=[b_pE], writes=[b_eb])
                S.op("act", lambda e: e.activation(out=enbt[:], in_=pE[:, 256:512], func=AF.Exp, scale=1.0 / 16), pr=[b_pE], writes=[b_enb])
                S.op("act", lambda e: e.activation(out=edt[:], in_=pF[:, 0:256], func=AF.Exp, scale=-1.0 / 16), pr=[b_pF], writes=[b_ed])
                S.op("act", lambda e: e.activation(out=eblt[:, :, 0:nch], in_=pF[:, 256:272].rearrange("p (h c) -> p h c", h=2)[:, :, 0:nch],
                                                   func=AF.Exp, scale=-1.0 / 16), pr=[b_pF], writes=[b_ebl])
                S.op("act", lambda e: e.activation(out=sg[:], in_=pC[:], func=AF.Silu), pr=[b_pC], writes=[b_sg])
                S.op("dve", lambda e: e.scalar_tensor_tensor(out=qin[:], in0=pA[:, 0:256], scalar=SCALE_G, in1=ebt[:], op0=ALU.mult, op1=ALU.mult),
                     reads=[b_eb], pr=[b_pA], writes=[b_qin])
                S.op("dve", lambda e: e.tensor_tensor(out=kin[:], in0=pA[:, 256:512], in1=enbt[:], op=ALU.mult), reads=[b_enb], pr=[b_pA], writes=[b_kin])
                S.op("dve", lambda e: e.tensor_tensor(out=kend[:], in0=pA[:, 256:512], in1=edt[:], op=ALU.mult), reads=[b_ed], pr=[b_pA], writes=[b_kend])
                for hh in range(2):
                    S.op("pe", lambda e, hh=hh: e.matmul(pG[:, hh * 128:(hh + 1) * 128], qin[:, hh * 128:(hh + 1) * 128], ident[:], start=True, stop=True),
                         reads=[b_qin, b_ident], writes=[b_pG])
                    S.op("pe", lambda e, hh=hh: e.matmul(pG[:, 256 + hh * 128:256 + (hh + 1) * 128], kin[:, hh * 128:(hh + 1) * 128], ident[:], start=True, stop=True),
                         reads=[b_kin, b_ident], writes=[b_pG])
                g4 = pG[:].rearrange("p (a h t) -> p a h t", a=2, h=2)
                S.op("dve", lambda e: e.tensor_copy(out=qTf[:], in_=g4[:, 0, :, :]), pr=[b_pG], writes=[b_qTf])
                S.op("dve", lambda e: e.tensor_copy(out=kTf[:], in_=g4[:, 1, :, :]), pr=[b_pG], writes=[b_kTf])
                for j in range(nch):
                    t, b_t = qtm[j]
                    S.op("dve", lambda e, t=t, j=j: e.tensor_copy(out=t[:, :, j * L:(j + 1) * L], in_=g4[:, 0, :, j * L:(j + 1) * L]), pr=[b_pG], writes=[b_t])
                for hh in range(2):
                    S.op("pe", lambda e, hh=hh: e.matmul(pD[:, 256:384], kTf[:, hh, :], qTf[:, hh, :], start=(hh == 0), stop=(hh == 1)),
                         reads=[b_kTf, b_qTf], writes=[b_pD])
                S.op("dve", lambda e: e.tensor_tensor(out=aTm[:], in0=pD[:, 256:384], in1=tri[:], op=ALU.mult), reads=[b_gc], pr=[b_pD], writes=[b_aTm])
                S.op("pe", lambda e: e.matmul(pO[:], aTm[:], vb[:], start=True, stop=False), reads=[b_aTm, b_vb], writes=[b_pO])
                for j in range(nch):
                    if state["carry"]:
                        Sf, b_Sf, Sb, b_Sb = state["S"]
                    else:
                        Sf, b_Sf = Sfs.next()
                        Sb, b_Sb = Sbs.next()
                        S.dma("sp", Sf[:], state_s[j].rearrange("(h p) v -> p h v", p=128), writes=[b_Sf])
                        S.op("act", lambda e, Sf=Sf, Sb=Sb: e.copy(out=Sb[:], in_=Sf[:]), reads=[b_Sf], writes=[b_Sb])
                    t, b_t = qtm[j]
                    for hh in range(2):
                        S.op("pe", lambda e, hh=hh, t=t, Sb=Sb: e.matmul(pO[:], t[:, hh, :], Sb[:, hh, :], start=False, stop=(j == nch - 1 and hh == 1)),
                             reads=[b_t, b_Sb], writes=[b_pO])
                    km, b_km = kms.next()
                    S.op("dve", lambda e, km=km, j=j: e.tensor_scalar(out=km[:], in0=kend[:], scalar1=rm[:, j:j + 1], scalar2=None, op0=ALU.mult),
                         reads=[b_kend, b_gc], writes=[b_km])
                    for hh, (pp, b_pp) in enumerate(((pA, b_pA), (pB, b_pB))):
                        S.op("pe", lambda e, hh=hh, pp=pp, km=km: e.matmul(pp[:], km[:, hh * 128:(hh + 1) * 128], vb[:], start=True, stop=True),
                             reads=[b_km, b_vb], writes=[b_pp])
                        S.op("dve", lambda e, hh=hh, pp=pp, Sf=Sf, j=j: e.scalar_tensor_tensor(out=Sf[:, hh, :], in0=Sf[:, hh, :], scalar=eblt[:, hh, j:j + 1],
                                                                                          in1=pp[:], op0=ALU.mult, op1=ALU.add),
                             reads=[b_ebl], pr=[b_pp], writes=[b_Sf])
                    if state["carry"]:
                        S.op("act", lambda e, Sf=Sf, Sb=Sb: e.copy(out=Sb[:], in_=Sf[:]), reads=[b_Sf], writes=[b_Sb])
                    else:
                        S.dma("sp", gla_s[j].rearrange("(h p) v -> p h v", p=128), Sf[:], reads=[b_Sf])
                ob, b_ob = obs.next()
                S.op("act", lambda e: e.copy(out=of_[:], in_=pO[:]), pr=[b_pO], writes=[b_of])
                S.op("dve", lambda e: e.tensor_tensor(out=osq[:], in0=of_[:], in1=of_[:], op=ALU.mult), reads=[b_of], writes=[b_osq])
                S.op("dve", lambda e: e.tensor_reduce(out=orr[:, 0:1], in_=osq[:], axis=AX.X, op=ALU.add), reads=[b_osq], writes=[b_orr])
                S.op("act", lambda e: e.activation(out=orr[:, 1:2], in_=orr[:, 0:1], func=AF.Ln, scale=1.0 / 512, bias=eps_g[:]), reads=[b_orr, b_gc], writes=[b_orr])
                S.op("act", lambda e: e.activation(out=orr[:, 2:3], in_=orr[:, 1:2], func=AF.Exp, scale=-0.5), reads=[b_orr], writes=[b_orr])
                S.op("dve", lambda e: e.scalar_tensor_tensor(out=of_[:], in0=of_[:], scalar=orr[:, 2:3], in1=ggt[:], op0=ALU.mult, op1=ALU.mult),
                     reads=[b_orr, b_gc, b_osq], writes=[b_of])
                S.op("dve", lambda e: e.tensor_tensor(out=ob[:], in0=of_[:], in1=sg[:], op=ALU.mult), reads=[b_of, b_sg], writes=[b_ob])
                S.dma("sp", ag_g[ltok0:ltok0 + 128, :], ob[:], reads=[b_ob])

            Sf0, b_Sf0 = Sfs.next()
            Sb0, b_Sb0 = Sbs.next()
            S.op("pool", lambda e: e.memset(Sf0[:], 0.0), writes=[b_Sf0])
            S.op("pool", lambda e: e.memset(Sb0[:], 0.0), writes=[b_Sb0])
            pstate = {"carry": True, "S": (Sf0, b_Sf0, Sb0, b_Sb0)}
            for t0 in range(0, SEQ, 256):
                xt, b_xt = xgs.next()
                for i in range(4):
                    S.dma("pool", xt[:, 8 * i:8 * i + 8, :], xgsrc[:, 8 * i:8 * i + 8, t0:t0 + 256], writes=[b_xt])
                for j in range(2):
                    gla_tile(xt, b_xt, j * 128, 64, t0 + j * 128, pstate)
            S.dma("sp", gla_p.rearrange("(h p) v -> p h v", p=128), Sf0[:], reads=[b_Sf0])
            xt, b_xt = xgs.next()
            for i in range(4):
                S.dma("pool", xt[:, 8 * i:8 * i + 8, 0:128], xgsrc[:, 8 * i:8 * i + 8, SEQ:SEQ + 128], writes=[b_xt])
            S.barrier()
            gla_tile(xt, b_xt, 0, 16, SEQ, {"carry": False})
        S.barrier()
        if stop <= 3:
            if debug:
                S.dma("sp", dbg_ag, ag_in)
                S.barrier()
            return nc
        ccsem = es0.enter_context(nc.semaphore("ccsem"))
        nc.gpsimd.collective_compute("AllGather", ALU.bypass, replica_groups=[list(range(NCORE))],
                                     ins=[ag_in.opt()], outs=[ag_out.opt()]).then_inc(ccsem, 1)
        nc.gpsimd.wait_ge(ccsem, 1)
        ag512 = ag_out.rearrange("(r two) d -> r (two d)", two=2)
        I32 = mybir.dt.int32
        git = sb(nc, es0, "git", [128, NT * 12], I32)
        b_git = Buf()
        S.dma("sp", git[:], gidx, writes=[b_git])
        epsc = sb(nc, es0, "epsc", [128, 1], F32)
        S.op("pool", lambda e: e.memset(epsc[:], EPS), writes=[b_const])

        def transpose_rows(src, b_src, dst_fn, b_dst, ptrs):
            for q in range(8):
                ptr, b_ptr = ptrs.next()
                for a in range(4):
                    fc = q * 4 + a
                    S.op("pe", lambda e: e.matmul(ptr[:, a * 128:(a + 1) * 128], src[:, fc * 128:(fc + 1) * 128], ident[:], start=True, stop=True),
                         reads=[b_src, b_ident], writes=[b_ptr])
                eng = "act" if q % 2 == 0 else "dve"
                if eng == "act":
                    S.op("act", lambda e: e.copy(out=dst_fn(q), in_=ptr[:].rearrange("p (a t) -> p a t", a=4)), pr=[b_ptr], writes=[b_dst])
                else:
                    S.op("dve", lambda e: e.tensor_copy(out=dst_fn(q), in_=ptr[:].rearrange("p (a t) -> p a t", a=4)), pr=[b_ptr], writes=[b_dst])

        def ln_pass(grow, brow, to_x1):
            with ExitStack() as es:
                gt_ = sb(nc, es, "ln_g", [128, D], F32)
                bt_ = sb(nc, es, "ln_b", [128, D], F32)
                b_gb = Buf()
                S.dma("sp", gt_[:], bcast_rows(lnp[grow:grow + 1, :], 128), writes=[b_gb])
                S.dma("sp", bt_[:], bcast_rows(lnp[brow:brow + 1, :], 128), writes=[b_gb])
                hs = rot_sb(nc, es, "ln_h", [128, D], F32, 2)
                junk = sb(nc, es, "ln_junk", [128, D], BF16); b_junk = Buf()
                st = rot_sb(nc, es, "ln_st", [128, 8], F32, 2)
                if to_x1:
                    xbs = rot_sb(nc, es, "ln_xb", [128, D], BF16, 2)
                    xTs = rot_sb(nc, es, "ln_xT", [128, 32, 128], BF16, 2)
                    ptrs = rot_ps(nc, es, "ln_ptr", 4)
                for tt in range(NT):
                    h, b_h = hs.next()
                    s_, b_s = st.next()
                    S.dma("sp", h[:], h_s[tt * 128:(tt + 1) * 128, :], writes=[b_h])
                    S.op("dve", lambda e: e.tensor_reduce(out=s_[:, 0:1], in_=h[:], axis=AX.X, op=ALU.add), reads=[b_h], writes=[b_s])
                    S.op("dve", lambda e: e.scalar_tensor_tensor(out=junk[:], in0=h[:], scalar=1.0, in1=h[:], op0=ALU.mult, op1=ALU.mult,
                                                                 accum_out=s_[:, 1:2]), reads=[b_h, b_s], writes=[b_junk, b_s])
                    S.op("dve", lambda e: e.tensor_scalar(out=s_[:, 2:3], in0=s_[:, 0:1], scalar1=-1.0 / D, scalar2=None, op0=ALU.mult), reads=[b_s], writes=[b_s])
                    S.op("dve", lambda e: e.tensor_tensor(out=s_[:, 3:4], in0=s_[:, 2:3], in1=s_[:, 2:3], op=ALU.mult), reads=[b_s], writes=[b_s])
                    S.op("dve", lambda e: e.scalar_tensor_tensor(out=s_[:, 4:5], in0=s_[:, 1:2], scalar=1.0 / D, in1=s_[:, 3:4], op0=ALU.mult, op1=ALU.subtract),
                         reads=[b_s], writes=[b_s])
                    S.op("act", lambda e: e.activation(out=s_[:, 5:6], in_=s_[:, 4:5], func=AF.Ln, bias=epsc[:], scale=1.0), reads=[b_s, b_const], writes=[b_s])
                    S.op("act", lambda e: e.activation(out=s_[:, 6:7], in_=s_[:, 5:6], func=AF.Exp, scale=-0.5), reads=[b_s], writes=[b_s])
                    S.op("dve", lambda e: e.tensor_scalar(out=h[:], in0=h[:], scalar1=s_[:, 2:3], scalar2=s_[:, 6:7], op0=ALU.add, op1=ALU.mult),
                         reads=[b_s, b_junk], writes=[b_h])
                    S.op("pool", lambda e: e.tensor_tensor(out=h[:], in0=h[:], in1=gt_[:], op=ALU.mult), reads=[b_gb], writes=[b_h])
                    S.op("dve", lambda e: e.tensor_tensor(out=h[:], in0=h[:], in1=bt_[:], op=ALU.add), reads=[b_gb], writes=[b_h])
                    if to_x1:
                        S.dma("sp", x1_s[tt * 128:(tt + 1) * 128, :], h[:], reads=[b_h])
                        xb, b_xb = xbs.next()
                        xT_, b_xT = xTs.next()
                        S.op("act", lambda e: e.copy(out=xb[:], in_=h[:]), reads=[b_h], writes=[b_xb])
                        transpose_rows(xb, b_xb, lambda q: xT_[:, q * 4:(q + 1) * 4, :], b_xT, ptrs)
                        S.dma("sp", x1T_s[:, tt * 128:(tt + 1) * 128].rearrange("(fc p) t -> p fc t", p=128), xT_[:], reads=[b_xT])
                    else:
                        S.dma("sp", y_own[tt * 128:(tt + 1) * 128, :], h[:], reads=[b_h])

        NH = (NT + 1) // 2
        for half in range(2):
            tiles = list(range(half * NH, min(NT, (half + 1) * NH)))
            with ExitStack() as es:
                mixT = sb(nc, es, "mixT", [128, NH, 32, 128], BF16)
                b_mixT = [Buf() for _ in range(NH)]
                mts = rot_sb(nc, es, "mt", [128, D], BF16, 2)
                ptrs = rot_ps(nc, es, "p4tr", 3)
                pws = rot_ps(nc, es, "p4w", 3)
                wsl = rot_sb(nc, es, "wsl", [128, 32, 256], BF16, 2)
                xos = rot_sb(nc, es, "xo", [128, 256], F32, 3)
                hts = rot_sb(nc, es, "ht", [128, 256], F32, 3)
                for li, tt in enumerate(tiles):
                    mt, b_mt = mts.next()
                    for hh in range(8):
                        S.gather(mt[:, hh * 256:(hh + 1) * 256], ag_out, git[:, tt * 12 + hh:tt * 12 + hh + 1], reads=[b_git], writes=[b_mt])
                    for gg in range(4):
                        S.gather(mt[:, 2048 + gg * 512:2048 + (gg + 1) * 512], ag512, git[:, tt * 12 + 8 + gg:tt * 12 + 9 + gg],
                                 reads=[b_git], writes=[b_mt])
                    transpose_rows(mt, b_mt, lambda q, li=li: mixT[:, li, q * 4:(q + 1) * 4, :], b_mixT[li], ptrs)
                wosrc = w_out.rearrange("(fc p) n -> p fc n", p=128)
                for ds_ in range(D // 256):
                    ws, b_ws = wsl.next()
                    for i in range(4):
                        S.dma("pool", ws[:, 8 * i:8 * i + 8, :], wosrc[:, 8 * i:8 * i + 8, ds_ * 256:(ds_ + 1) * 256], writes=[b_ws])
                    for li, tt in enumerate(tiles):
                        pw, b_pw = pws.next()
                        xo, b_xo = xos.next()
                        ht, b_ht = hts.next()
                        S.dma("sp", xo[:], x_own[tt * 128:(tt + 1) * 128, ds_ * 256:(ds_ + 1) * 256], writes=[b_xo])
                        for fc in range(32):
                            S.op("pe", lambda e: e.matmul(pw[:, 0:256], mixT[:, li, fc, :], ws[:, fc, :], start=(fc == 0), stop=(fc == 31)),
                                 reads=[b_mixT[li], b_ws], writes=[b_pw])
                        S.op("dve", lambda e: e.scalar_tensor_tensor(out=ht[:], in0=xo[:], scalar=ALPHA, in1=pw[:, 0:256], op0=ALU.mult, op1=ALU.add),
                             reads=[b_xo], pr=[b_pw], writes=[b_ht])
                        S.dma("sp", h_s[tt * 128:(tt + 1) * 128, ds_ * 256:(ds_ + 1) * 256], ht[:], reads=[b_ht])
            S.barrier()
        ln_pass(0, 1, True)
        S.barrier()
        if stop <= 4:
            if debug:
                S.dma("sp", dbg_x1, x1_s)
                S.barrier()
            return nc

        groups = [list(range(g0, min(NT, g0 + 4))) for g0 in range(0, NT, 4)]
        x1Tv = x1T_s.rearrange("(fc p) t -> p fc t", p=128)

        with ExitStack() as es:
            k1b = sb(nc, es, "k1b", [128, 8, 128], BF16)
            k2b = sb(nc, es, "k2b", [128, 8, 128], BF16)
            b_kb = Buf()
            S.dma("pool", k1b[:], k1T, writes=[b_kb])
            S.dma("pool", k2b[:], k2T, writes=[b_kb])
            xgs5 = rot_sb(nc, es, "xg5", [128, 32, 512], BF16, 2)
            wqs = rot_sb(nc, es, "wqb", [128, 32, 128], BF16, 2)
            qT = sb(nc, es, "qT", [128, 16, 512], BF16); b_qT = Buf()
            pqs = rot_ps(nc, es, "p5q", 2)
            psc = [ps(nc, es, "p5s%d" % i, [128, 512]) for i in range(4)]
            b_psc = [Buf() for _ in range(4)]
            s1t = sb(nc, es, "s1t", [128, 8, 128], F32); b_s1 = Buf()
            gts = rot_sb(nc, es, "gt5", [128, 2056], F32, 2)
            t1 = sb(nc, es, "t1", [128, 8, 16], F32); b_t1 = Buf()
            t2 = sb(nc, es, "t2", [128, 8, 16], F32); b_t2 = Buf()
            tmp = sb(nc, es, "tk_tmp", [128, 128], F32); b_tmp = Buf()
            cand = sb(nc, es, "cand", [128, 16, 16], F32); b_cand = Buf()
            tmp2 = sb(nc, es, "tk_tmp2", [128, 256], F32); b_tmp2 = Buf()
            Bt = sb(nc, es, "Bt", [128, 8, 16], F32); b_B = Buf()
            Bm = sb(nc, es, "Bm", [128, 8, 16], F32); b_Bm = Buf()
            sm = sb(nc, es, "sm", [128, 4, 8], F32); b_sm = Buf()
            wqsrc = peer_wq.rearrange("(fc p) n -> p fc n", p=128)

            def top16(src2d, dst16, b_src, b_dst, scratch, b_scr):
                S.op("dve", lambda e: e.max(out=dst16[:, 0:8], in_=src2d), reads=[b_src], writes=[b_dst])
                S.op("dve", lambda e: e.match_replace(out=scratch, in_to_replace=dst16[:, 0:8], in_values=src2d, imm_value=-1e30),
                     reads=[b_src, b_dst], writes=[b_scr])
                S.op("dve", lambda e: e.max(out=dst16[:, 8:16], in_=scratch), reads=[b_scr], writes=[b_dst])

            for grp in groups:
                N = 128 * len(grp)
                c0 = grp[0] * 128
                xg, b_xg = xgs5.next()
                for i in range(4):
                    S.dma("sp", xg[:, 8 * i:8 * i + 8, 0:N], x1Tv[:, 8 * i:8 * i + 8, c0:c0 + N], writes=[b_xg])
                for blk in range(16):
                    wqb, b_wq = wqs.next()
                    S.dma("pool", wqb[:], wqsrc[:, :, blk * 128:(blk + 1) * 128], writes=[b_wq])
                    pq, b_pq = pqs.next()
                    for fc in range(32):
                        S.op("pe", lambda e: e.matmul(pq[:, 0:N], wqb[:, fc, :], xg[:, fc, 0:N], start=(fc == 0), stop=(fc == 31)),
                             reads=[b_wq, b_xg], writes=[b_pq])
                    S.op("act", lambda e: e.copy(out=qT[:, blk, 0:N], in_=pq[:, 0:N]), pr=[b_pq], writes=[b_qT])
                for li, tt in enumerate(grp):
                    gt, b_gt = gts.next()
                    tk = slice(li * 128, (li + 1) * 128)
                    for hh in range(8):
                        S.op("pe", lambda e: e.matmul(psc[hh // 4][:, (hh % 4) * 128:(hh % 4 + 1) * 128], qT[:, 2 * hh, tk], k1b[:, hh, :], start=True, stop=True),
                             reads=[b_qT, b_kb], writes=[b_psc[hh // 4]])
                        S.op("pe", lambda e: e.matmul(psc[2 + hh // 4][:, (hh % 4) * 128:(hh % 4 + 1) * 128], qT[:, 2 * hh + 1, tk], k2b[:, hh, :], start=True, stop=True),
                             reads=[b_qT, b_kb], writes=[b_psc[2 + hh // 4]])
                    s2v = gt[:, 1024:2048].rearrange("p (h k) -> p h k", h=8)
                    for i in range(2):
                        S.op("act", lambda e: e.copy(out=s1t[:, 4 * i:4 * i + 4, :], in_=psc[i][:].rearrange("p (h k) -> p h k", h=4)), pr=[b_psc[i]], writes=[b_s1])
                        S.op("act", lambda e: e.copy(out=s2v[:, 4 * i:4 * i + 4, :], in_=psc[2 + i][:].rearrange("p (h k) -> p h k", h=4)), pr=[b_psc[2 + i]], writes=[b_gt])
                    for hh in range(8):
                        top16(s1t[:, hh, :], t1[:, hh, :], b_s1, b_t1, tmp[:], b_tmp)
                        top16(s2v[:, hh, :], t2[:, hh, :], b_gt, b_t2, tmp[:], b_tmp)
                        S.op("dve", lambda e: e.tensor_tensor(out=cand[:], in0=t1[:, hh, :].unsqueeze(2).broadcast_to([128, 16, 16]),
                                                               in1=t2[:, hh, :].unsqueeze(1).broadcast_to([128, 16, 16]), op=ALU.add),
                             reads=[b_t1, b_t2], writes=[b_cand])
                        top16(cand[:].rearrange("p a b -> p (a b)"), Bt[:, hh, :], b_cand, b_B, tmp2[:], b_tmp2)
                    S.op("dve", lambda e: e.tensor_tensor(out=Bm[:], in0=Bt[:], in1=Bt[:, :, 0:1].broadcast_to([128, 8, 16]), op=ALU.subtract),
                         reads=[b_B], writes=[b_Bm])
                    S.op("act", lambda e: e.activation(out=Bm[:], in_=Bm[:], func=AF.Exp), reads=[b_Bm], writes=[b_Bm])
                    S.op("dve", lambda e: e.tensor_reduce(out=sm[:, 0, :], in_=Bm[:], axis=AX.X, op=ALU.add), reads=[b_Bm], writes=[b_sm])
                    S.op("act", lambda e: e.activation(out=sm[:, 1, :], in_=sm[:, 0, :], func=AF.Ln), reads=[b_sm], writes=[b_sm])
                    S.op("dve", lambda e: e.tensor_tensor(out=sm[:, 2, :], in0=Bt[:, :, 15], in1=Bt[:, :, 0], op=ALU.subtract), reads=[b_B, b_sm], writes=[b_sm])
                    S.op("dve", lambda e: e.tensor_tensor(out=gt[:, 2048:2056], in0=sm[:, 2, :], in1=sm[:, 1, :], op=ALU.subtract), reads=[b_sm], writes=[b_gt])
                    S.op("dve", lambda e: e.tensor_scalar(out=sm[:, 3, :], in0=Bt[:, :, 15], scalar1=-1e-5, scalar2=None, op0=ALU.add), reads=[b_B, b_sm], writes=[b_sm])
                    S.op("dve", lambda e: e.tensor_tensor(out=gt[:, 0:1024].rearrange("p (h k) -> p h k", h=8), in0=s1t[:],
                                                           in1=sm[:, 3, :].unsqueeze(2).broadcast_to([128, 8, 128]), op=ALU.subtract),
                         reads=[b_s1, b_sm], writes=[b_gt])
                    S.dma("sp", gate_s[tt], gt[:], reads=[b_gt])
        S.barrier()
        if stop <= 5:
            if debug:
                S.dma("sp", dbg_gate, gate_s)
                S.barrier()
            return nc

        with ExitStack() as es:
            xg6 = sb(nc, es, "xg6", [128, 32, 512], BF16); b_xg6 = Buf()
            gate6 = sb(nc, es, "gate6", [128, 4, 2056], F32); b_gate6 = Buf()
            uts = rot_sb(nc, es, "ut", [128, 32, 256], BF16, 2)
            spps = rot_sb(nc, es, "spp", [128, 8, 2, 128], F32, 4)
            Ets = rot_sb(nc, es, "Et", [128, 8, 2, 128], BF16, 4)
            Gs = rot_sb(nc, es, "G6", [128, 8, 2, 128], BF16, 4)
            gels = rot_sb(nc, es, "gel", [128, 512], F32, 2)
            cts = rot_sb(nc, es, "ct6", [128, 512], BF16, 2)
            pAe = [rot_ps(nc, es, "p6a%d" % i, 2) for i in range(2)]
            pGe = [rot_ps(nc, es, "p6g%d" % i, 2) for i in range(2)]
            usrc = uT.rearrange("(fc p) e -> p fc e", p=128)
            for grp in groups:
                N = 128 * len(grp)
                c0 = grp[0] * 128
                for i in range(4):
                    S.dma("sp", xg6[:, 8 * i:8 * i + 8, 0:N], x1Tv[:, 8 * i:8 * i + 8, c0:c0 + N], writes=[b_xg6])
                for li, tt in enumerate(grp):
                    S.dma("sp", gate6[:, li, :], gate_s[tt], writes=[b_gate6])
                for ep in range(NEXP // 256):
                    e0 = ep * 256
                    ut, b_ut = uts.next()
                    for i in range(4):
                        S.dma("pool", ut[:, 8 * i:8 * i + 8, :], usrc[:, 8 * i:8 * i + 8, e0:e0 + 256], writes=[b_ut])
                    pa = [pAe[0].next(), pAe[1].next()]
                    pg = [pGe[0].next(), pGe[1].next()]
                    for eb in range(2):
                        for fc in range(32):
                            S.op("pe", lambda e: e.matmul(pa[eb][0][:, 0:N], ut[:, fc, eb * 128:(eb + 1) * 128], xg6[:, fc, 0:N], start=(fc == 0), stop=(fc == 31)),
                                 reads=[b_ut, b_xg6], writes=[pa[eb][1]])
                    for li, tt in enumerate(grp):
                        spp, b_spp = spps.next()
                        Et, b_Et = Ets.next()
                        G, b_G = Gs.next()
                        s1v = gate6[:, li, 0:1024].rearrange("p (h k) -> p h k", h=8)[:, :, 2 * ep:2 * ep + 2]
                        s2v = gate6[:, li, 1024:2048].rearrange("p (h k) -> p h k", h=8)
                        S.op("pool", lambda e: e.tensor_tensor(out=spp[:], in0=s2v.unsqueeze(2).broadcast_to([128, 8, 2, 128]),
                                                                in1=s1v.unsqueeze(3).broadcast_to([128, 8, 2, 128]), op=ALU.add),
                             reads=[b_gate6], writes=[b_spp])
                        for hh in range(8):
                            S.op("act", lambda e: e.activation(out=Et[:, hh, :, :], in_=spp[:, hh, :, :], func=AF.Exp,
                                                               bias=gate6[:, li, 2048 + hh:2049 + hh], scale=1.0),
                                 reads=[b_spp, b_gate6], writes=[b_Et])
                        S.op("dve", lambda e: e.scalar_tensor_tensor(out=G[:].rearrange("p h a k -> p (h a k)"), in0=spp[:].rearrange("p h a k -> p (h a k)"),
                                                                     scalar=0.0, in1=Et[:].rearrange("p h a k -> p (h a k)"), op0=ALU.is_ge, op1=ALU.mult),
                             reads=[b_spp, b_Et], writes=[b_G])
                        g2 = G[:].rearrange("p h a k -> p h (a k)")
                        S.op("dve", lambda e: e.tensor_tensor(out=g2[:, 0:4, :], in0=g2[:, 0:4, :], in1=g2[:, 4:8, :], op=ALU.add), reads=[b_G], writes=[b_G])
                        S.op("dve", lambda e: e.tensor_tensor(out=g2[:, 0:2, :], in0=g2[:, 0:2, :], in1=g2[:, 2:4, :], op=ALU.add), reads=[b_G], writes=[b_G])
                        S.op("dve", lambda e: e.tensor_tensor(out=g2[:, 0, :], in0=g2[:, 0, :], in1=g2[:, 1, :], op=ALU.add), reads=[b_G], writes=[b_G])
                        for eb in range(2):
                            S.op("pe", lambda e: e.matmul(pg[eb][0][:, li * 128:(li + 1) * 128], G[:, 0, eb, :], ident[:], start=True, stop=True),
                                 reads=[b_G, b_ident], writes=[pg[eb][1]])
                    for eb in range(2):
                        gel, b_gel = gels.next()
                        ct, b_ct = cts.next()
                        S.op("act", lambda e: e.activation(out=gel[:, 0:N], in_=pa[eb][0][:, 0:N], func=AF.Gelu), pr=[pa[eb][1]], writes=[b_gel])
                        S.op("dve", lambda e: e.tensor_tensor(out=ct[:, 0:N], in0=gel[:, 0:N], in1=pg[eb][0][:, 0:N], op=ALU.mult),
                             reads=[b_gel], pr=[pg[eb][1]], writes=[b_ct])
                        S.dma("sp", coefT_s[e0 + eb * 128:e0 + (eb + 1) * 128, c0:c0 + N], ct[:, 0:N], reads=[b_ct])
        S.barrier()

        with ExitStack() as es:
            PT = 6
            lg = sb(nc, es, "ln2_g", [128, D], F32)
            lb = sb(nc, es, "ln2_b", [128, D], F32)
            b_gb = Buf()
            S.dma("sp", lg[:], bcast_rows(lnp[2:3, :], 128), writes=[b_gb])
            S.dma("sp", lb[:], bcast_rows(lnp[3:4, :], 128), writes=[b_gb])
            yacc = sb(nc, es, "yacc", [128, PT, D], F32)
            b_y = [Buf() for _ in range(PT)]
            c8s = rot_sb(nc, es, "c8", [128, 8, PT * 128], BF16, 2)
            v8s = rot_sb(nc, es, "v8", [128, 8, 512], BF16, 3)
            junk = sb(nc, es, "ln2_junk", [128, D], BF16); b_junk = Buf()
            st = rot_sb(nc, es, "ln2_st", [128, 8], F32, 2)
            accs = rot_ps(nc, es, "p6acc", 8)
            cTv = coefT_s.rearrange("(ec p) t -> p ec t", p=128)
            vv = v_tab.rearrange("(ec p) d -> p ec d", p=128)
            for p0 in range(0, NT, PT):
                tiles = list(range(p0, min(NT, p0 + PT)))
                N = 128 * len(tiles)
                c0 = tiles[0] * 128
                for li, tt in enumerate(tiles):
                    S.dma("sp", yacc[:, li, :], x1_s[tt * 128:(tt + 1) * 128, :], writes=[b_y[li]])
                    S.op("act", lambda e: e.mul(out=yacc[:, li, :], in_=yacc[:, li, :], mul=ALPHA), writes=[b_y[li]])
                for e8 in range(NEXP // 1024):
                    c8, b_c8 = c8s.next()
                    S.dma("sp", c8[:, :, 0:N], cTv[:, e8 * 8:(e8 + 1) * 8, c0:c0 + N], writes=[b_c8])
                    for ds_ in range(D // 512):
                        v8, b_v8 = v8s.next()
                        S.dma("pool", v8[:], vv[:, e8 * 8:(e8 + 1) * 8, ds_ * 512:(ds_ + 1) * 512], writes=[b_v8])
                        for li in range(len(tiles)):
                            acc, b_acc = accs.next()
                            for k in range(8):
                                S.op("pe", lambda e: e.matmul(acc[:], c8[:, k, li * 128:(li + 1) * 128], v8[:, k, :], start=(k == 0), stop=(k == 7)),
                                     reads=[b_c8, b_v8], writes=[b_acc])
                            ysl = yacc[:, li, ds_ * 512:(ds_ + 1) * 512]
                            S.op("dve", lambda e: e.tensor_tensor(out=ysl, in0=ysl, in1=acc[:], op=ALU.add), pr=[b_acc], writes=[b_y[li]])
                for li, tt in enumerate(tiles):
                    h = yacc[:, li, :]
                    b_h = b_y[li]
                    s_, b_s = st.next()
                    S.op("dve", lambda e: e.tensor_reduce(out=s_[:, 0:1], in_=h, axis=AX.X, op=ALU.add), reads=[b_h], writes=[b_s])
                    S.op("dve", lambda e: e.scalar_tensor_tensor(out=junk[:], in0=h, scalar=1.0, in1=h, op0=ALU.mult, op1=ALU.mult,
                                                                 accum_out=s_[:, 1:2]), reads=[b_h, b_s], writes=[b_junk, b_s])
                    S.op("dve", lambda e: e.tensor_scalar(out=s_[:, 2:3], in0=s_[:, 0:1], scalar1=-1.0 / D, scalar2=None, op0=ALU.mult), reads=[b_s], writes=[b_s])
                    S.op("dve", lambda e: e.tensor_tensor(out=s_[:, 3:4], in0=s_[:, 2:3], in1=s_[:, 2:3], op=ALU.mult), reads=[b_s], writes=[b_s])
                    S.op("dve", lambda e: e.scalar_tensor_tensor(out=s_[:, 4:5], in0=s_[:, 1:2], scalar=1.0 / D, in1=s_[:, 3:4], op0=ALU.mult, op1=ALU.subtract),
                         reads=[b_s], writes=[b_s])
                    S.op("act", lambda e: e.activation(out=s_[:, 5:6], in_=s_[:, 4:5], func=AF.Ln, bias=epsc[:], scale=1.0), reads=[b_s, b_const], writes=[b_s])
                    S.op("act", lambda e: e.activation(out=s_[:, 6:7], in_=s_[:, 5:6], func=AF.Exp, scale=-0.5), reads=[b_s], writes=[b_s])
                    S.op("dve", lambda e: e.tensor_scalar(out=h, in0=h, scalar1=s_[:, 2:3], scalar2=s_[:, 6:7], op0=ALU.add, op1=ALU.mult),
                         reads=[b_s, b_junk], writes=[b_h])
                    S.op("pool", lambda e: e.tensor_tensor(out=h, in0=h, in1=lg[:], op=ALU.mult), reads=[b_gb], writes=[b_h])
                    S.op("dve", lambda e: e.tensor_tensor(out=h, in0=h, in1=lb[:], op=ALU.add), reads=[b_gb], writes=[b_h])
                    S.dma("sp", y_own[tt * 128:(tt + 1) * 128, :], h, reads=[b_h])
        S.barrier()
    return nc


def _rope_table(SEQ):
    inv = (np.float32(10000.0) ** (-np.arange(0, 128, 2, dtype=np.float32) / np.float32(128))).astype(np.float32)
    pos = np.concatenate([np.arange(SEQ), np.arange(SEQ), np.tile(PAST + np.arange(TS), NSB)]).astype(np.float32)
    ang = (pos[:, None] * inv[None, :]).astype(np.float32)
    return np.ascontiguousarray(np.concatenate([np.cos(ang), np.sin(ang)], axis=1).astype(np.float32))


def prep(inp, SEQ):
    NP = 2 * SEQ
    NTOK = NP + NS
    TPC = NP // NCORE
    SPC = NS // NCORE
    NGL = SEQ + NS // 2
    NT = TPC // 128 + 1
    RAG = NTOK + 2 * NGL
    f = lambda a: np.ascontiguousarray(np.asarray(a, dtype=np.float32))
    xall = np.concatenate([np.asarray(inp["x_prompt"]).reshape(NP, D), np.asarray(inp["x_sample"]).reshape(NS, D)], axis=0)
    xT = f(xall.T)
    cs = _rope_table(SEQ)
    w_in = np.asarray(inp["w_in"])[0]
    lam4 = f(np.concatenate([np.asarray(inp[k])[0] for k in ("lam_q1", "lam_k1", "lam_q2", "lam_k2")])[None, :])
    dng = f(np.asarray(inp["diff_norm_g"])[0][None, :])
    gng = f(np.asarray(inp["gla_norm_g"])[0][None, :])
    ck = np.asarray(inp["cache_diff_k"])[0]
    cv = np.asarray(inp["cache_diff_v"])[0]
    wg2_all = np.asarray(inp["w_gate2"])[0]
    bg_all = np.asarray(inp["b_gate"])[0]
    st_all = np.asarray(inp["state_gla"])[0]
    w_out = f(np.asarray(inp["w_out"])[0])
    lnp = f(np.stack([np.asarray(inp[k])[0] for k in ("ln1_g", "ln1_b", "ln2_g", "ln2_b")]))
    wq = f(np.asarray(inp["peer_wq"])[0])
    k1T = f(np.asarray(inp["peer_keys1"])[0].transpose(2, 0, 1))
    k2T = f(np.asarray(inp["peer_keys2"])[0].transpose(2, 0, 1))
    uT = f(np.asarray(inp["peer_u"])[0].T)
    v_tab = f(np.asarray(inp["peer_v"])[0])
    xTg = []
    for bb in range(2):
        xTg.append(f(np.concatenate([xT[:, bb * SEQ:(bb + 1) * SEQ], xT[:, NP + bb * 128:NP + bb * 128 + 128]], axis=1)))
    maps = []
    for c in range(NCORE):
        g, bb = c // 2, c % 2
        cols = np.concatenate([np.arange(c * 256, c * 256 + 256), 2048 + np.arange(c * 256, c * 256 + 256),
                               4096 + np.arange(c * 256, c * 256 + 256)])
        gcols = np.concatenate([6144 + np.arange(g * 256, g * 256 + 256), 7168 + np.arange(g * 256, g * 256 + 256),
                                8192 + np.arange(g * 512, g * 512 + 512), 10240 + np.arange(g * 512, g * 512 + 512),
                                12288 + np.arange(16)])
        x_own = np.zeros((NT * 128, D), np.float32)
        x_own[:TPC] = xall[c * TPC:(c + 1) * TPC]
        x_own[TPC:TPC + SPC] = xall[NP + c * SPC:NP + (c + 1) * SPC]
        gtok = np.zeros(NT * 128, np.int64)
        gtok[:TPC] = c * TPC + np.arange(TPC)
        gtok[TPC:TPC + SPC] = NP + c * SPC + np.arange(SPC)
        bbj = c // 4
        ltok = np.zeros(NT * 128, np.int64)
        ltok[:TPC] = gtok[:TPC] - bbj * SEQ
        ltok[TPC:TPC + SPC] = SEQ + (c * SPC + np.arange(SPC)) - bbj * 128
        gi = np.zeros((128, NT * 12), np.int32)
        for tt in range(NT):
            sl = slice(tt * 128, (tt + 1) * 128)
            for h in range(8):
                gi[:, tt * 12 + h] = h * RAG + gtok[sl]
            for gg in range(4):
                r = 2 * gg + bbj
                gi[:, tt * 12 + 8 + gg] = (r * RAG + NTOK) // 2 + ltok[sl]
        m = {
            "xT": xT, "cs_tab": cs, "w_diff": f(w_in[:, cols]), "lam4": lam4, "dng": dng,
            "cache_kT": f(ck[:, :, c, :].reshape(NSB, PAST, 2, 128).transpose(0, 2, 3, 1)),
            "cache_v": f(cv[:, :, c, :]),
            "xTg": xTg[bb], "w_gla": f(w_in[:, gcols]), "wg2": f(wg2_all[:, g * 256:(g + 1) * 256]),
            "bg": f(bg_all[g * 256:(g + 1) * 256][None, :]), "gng": gng,
            "state_s": f(st_all[bb * 8:(bb + 1) * 8, g]),
            "gidx": np.ascontiguousarray(gi), "x_own": x_own, "w_out": w_out, "lnp": lnp,
            "peer_wq": wq, "k1T": k1T, "k2T": k2T, "uT": uT, "v_tab": v_tab,
        }
        maps.append(m)
    return maps


_NC_CACHE = {}


def kernel(**inputs):
    SEQ = int(np.asarray(inputs["x_prompt"]).shape[1])
    NP = 2 * SEQ
    TPC = NP // NCORE
    SPC = NS // NCORE
    if SEQ not in _NC_CACHE:
        _NC_CACHE[SEQ] = build(SEQ)
    nc = _NC_CACHE[SEQ]
    maps = prep(inputs, SEQ)
    maps = [{k: m[k] for k in nc.in_names} for m in maps]
    res = run_bass_kernel_spmd(nc, maps, core_ids=list(range(NCORE)))
    R = res.results
    f = lambda a: np.asarray(a, dtype=np.float32)
    y_p = np.concatenate([f(R[c]["y_own"])[:TPC] for c in range(NCORE)], axis=0).reshape(2, SEQ, D)
    y_s = np.concatenate([f(R[c]["y_own"])[TPC:TPC + SPC] for c in range(NCORE)], axis=0).reshape(NSB, TS, D)
    k_all = np.stack([f(R[c]["k_out"]) for c in range(NCORE)], axis=1)
    v_all = np.stack([f(R[c]["v_out"]) for c in range(NCORE)], axis=1)
    k_p = np.ascontiguousarray(k_all[:NP]).reshape(1, 2, SEQ, 8, 256)
    v_p = np.ascontiguousarray(v_all[:NP]).reshape(1, 2, SEQ, 8, 256)
    k_s = np.ascontiguousarray(k_all[NP:]).reshape(1, NSB, TS, 8, 256)
    v_s = np.ascontiguousarray(v_all[NP:]).reshape(1, NSB, TS, 8, 256)
    g_p = np.zeros((1, 2, 4, 256, 512), np.float32)
    g_s = np.zeros((1, NSB, 4, 256, 512), np.float32)
    for c in range(NCORE):
        g, bb = c // 2, c % 2
        g_p[0, bb, g] = f(R[c]["gla_p"])
        g_s[0, bb * 8:(bb + 1) * 8, g] = f(R[c]["gla_s"])
    return (y_p, y_s, k_p, v_p, g_p, k_s, v_s, g_s)
```
